# Optimizing a Trainium2 kernel written in Bass

```python
import math
import jax, jax.numpy as jnp
from jax import lax
import numpy as np

D_MODEL = 2048
BATCH = 4
SEQ = 4096
DEPTH = 2
DEC_BATCH = 8
DEC_SEQ = 64
PAST_LEN = 4096

CHUNK = 64
N_AB = (DEPTH + 1) // 2
N_C = DEPTH // 2
D_A = D_MODEL // 2
G_A = 4
C_A = D_A // G_A
MLP_CHUNK = 128
D_B = D_MODEL // 2
HEAD_B = 64
H_B = D_B // HEAD_B
R_W = 64
R_A = 64
D_B_SHIFT = 3 * D_B + R_W + R_A
HEAD_C = 128
H_C = D_MODEL // (2 * HEAD_C)
D_C = H_C * 2 * HEAD_C
Q_BLOCK = 128
P_AB = 3 * D_A + D_B_SHIFT + D_B
P_C = 4 * D_C
DEEPNORM_ALPHA = (2 * DEPTH) ** 0.25
DEEPNORM_BETA = (8 * DEPTH) ** -0.25
LN_EPS = 1e-5
GN_EPS_B = 64e-5
SUBLN_EPS = 1e-5

kernel_name = "hybrid_streaming_gmlp_rwkv7_diffattn_step"


def layer_norm(x, g, b, eps=LN_EPS):
    xf = x.astype(jnp.float32)
    mu = jnp.mean(xf, -1, keepdims=True)
    var = jnp.mean(jnp.square(xf - mu), -1, keepdims=True)
    return ((xf - mu) * lax.rsqrt(var + eps) * g + b).astype(x.dtype)


def spatial_gate(v_n, ws, bs):
    b, t, _ = v_n.shape
    L = min(t, MLP_CHUNK)
    n = t // L
    mask = jnp.tril(jnp.ones((L, L), dtype=bool))
    w = jnp.where(mask, ws[:, :L, :L], 0)
    vg = v_n.reshape(b, n, L, G_A, C_A)
    bias = jnp.swapaxes(bs[:, :L], 0, 1)[None, None, :, :, None]
    out = jnp.einsum('gts,bnsgc->bntgc', w, vg) + bias
    return out.reshape(b, t, D_A)


def rwkv7_scan(s0, r, decay, k, v, kk, a):
    def step(s, inp):
        r_t, w_t, k_t, v_t, kk_t, a_t = inp
        s_kk = jnp.einsum('bhvk,bhk->bhv', s, kk_t)
        s = (s * w_t[:, :, None, :] - s_kk[..., None] * (kk_t * a_t)[:, :, None, :]
             + v_t[..., None] * k_t[:, :, None, :])
        return s, jnp.einsum('bhvk,bhk->bhv', s, r_t)
    xs = tuple(jnp.swapaxes(z.astype(jnp.float32), 0, 1) for z in (r, decay, k, v, kk, a))
    s_fin, y = lax.scan(step, s0.astype(jnp.float32), xs)
    return s_fin, jnp.swapaxes(y, 0, 1)


def ab_layer(x, shift_prev, wkv0, w_in, a_ln_g, a_ln_b, a_ws, a_bs, b_mu, b_w0, b_w2,
             b_a0, b_a2, b_kk, b_ka, b_rk, b_lnx_g, b_lnx_b, w_out, ln_g, ln_b):
    f32 = jnp.float32
    bsz, t, _ = x.shape
    proj = x @ w_in
    u, v_a, g_a, h_b, g_b = jnp.split(proj, [D_A, 2 * D_A, 3 * D_A, 3 * D_A + D_B_SHIFT], axis=-1)
    v_n = layer_norm(v_a, a_ln_g, a_ln_b)
    out_a = u * spatial_gate(v_n, a_ws, a_bs) * jax.nn.silu(g_a)
    prev = jnp.concatenate([shift_prev[:, None].astype(h_b.dtype), h_b[:, :-1]], axis=1)
    hs = h_b + (prev - h_b) * b_mu
    r, k, v, wl, al = jnp.split(hs, [D_B, 2 * D_B, 3 * D_B, 3 * D_B + R_W], axis=-1)
    w = -jax.nn.softplus(-(b_w0 + jnp.tanh(wl) @ b_w2).astype(f32)) - 0.5
    decay = jnp.exp(-jnp.exp(w))
    a = jax.nn.sigmoid((b_a0 + al @ b_a2).astype(f32))
    heads = lambda z: z.astype(f32).reshape(bsz, t, H_B, HEAD_B)
    kk = heads(k * b_kk)
    kk = kk / jnp.maximum(jnp.sqrt(jnp.sum(jnp.square(kk), -1, keepdims=True)), 1e-12)
    k_mod = k.astype(f32) * (1.0 + (a - 1.0) * b_ka)
    r_h, k_h, v_h, a_h, w_h = heads(r), heads(k_mod), heads(v), heads(a), heads(decay)
    wkv_new, y = rwkv7_scan(wkv0, r_h, w_h, k_h, v_h, kk, a_h)
    mu_y = jnp.mean(y, -1, keepdims=True)
    var_y = jnp.mean(jnp.square(y - mu_y), -1, keepdims=True)
    y = ((y - mu_y) * lax.rsqrt(var_y + GN_EPS_B)).reshape(bsz, t, D_B) * b_lnx_g + b_lnx_b
    bonus = (jnp.sum(r_h * k_h * b_rk, -1, keepdims=True) * v_h).reshape(bsz, t, D_B)
    out_b = ((y + bonus) * jax.nn.silu(g_b.astype(f32))).astype(x.dtype)
    out = jnp.concatenate([out_a, out_b], axis=-1) @ w_out
    x_new = layer_norm(DEEPNORM_ALPHA * x + out, ln_g, ln_b)
    return x_new, h_b[:, -1], wkv_new, v_n


def diff_attend(q, k, v, mask, lam):
    s = jnp.einsum('bqhjd,bkhjd->jbhqk', q, k).astype(jnp.float32) * (HEAD_C ** -0.5)
    if mask is not None:
        s = jnp.where(mask, s, -jnp.inf)
    p = jax.nn.softmax(s, axis=-1)
    attn = p[0] - lam * p[1]
    return jnp.einsum('bhqk,bkhe->bqhe', attn.astype(v.dtype), v)


def c_layer(x, k_cache, v_cache, lam_init, w_in, lq1, lk1, lq2, lk2, subln_g, w_out, ln_g, ln_b):
    f32 = jnp.float32
    bsz, t, _ = x.shape
    q, k, v, g = jnp.split(x @ w_in, 4, axis=-1)
    q = q.reshape(bsz, t, H_C, 2, HEAD_C)
    k = k.reshape(bsz, t, H_C, 2, HEAD_C)
    v = v.reshape(bsz, t, H_C, 2 * HEAD_C)
    lam = (jnp.exp(jnp.sum((lq1 * lk1).astype(f32))) - jnp.exp(jnp.sum((lq2 * lk2).astype(f32)))
           + lam_init)
    if k_cache is None:
        nb = t // Q_BLOCK
        qb = jnp.moveaxis(q.reshape(bsz, nb, Q_BLOCK, H_C, 2, HEAD_C), 1, 0)
        key_pos = jnp.arange(t)
        def block(args):
            q_blk, i = args
            q_pos = i * Q_BLOCK + jnp.arange(Q_BLOCK)
            limit = (q_pos // CHUNK + 1) * CHUNK
            return diff_attend(q_blk, k, v, key_pos[None, :] < limit[:, None], lam)
        o = lax.map(block, (qb, jnp.arange(nb)))
        o = jnp.moveaxis(o, 0, 1).reshape(bsz, t, H_C, 2 * HEAD_C)
    else:
        k_all = jnp.concatenate([k_cache.astype(k.dtype), k], axis=1)
        v_all = jnp.concatenate([v_cache.astype(v.dtype), v], axis=1)
        o = diff_attend(q, k_all, v_all, None, lam)
    of = o.astype(f32)
    of = of * lax.rsqrt(jnp.mean(jnp.square(of), -1, keepdims=True) + SUBLN_EPS) * subln_g
    of = of * (1.0 - lam_init)
    o = (of.reshape(bsz, t, D_C) * jax.nn.silu(g.astype(f32))).astype(x.dtype)
    out = o @ w_out
    return layer_norm(DEEPNORM_ALPHA * x + out, ln_g, ln_b), k, v


def setup_inputs(seed: int = 0) -> dict:
    key = jax.random.key(seed)
    ks = iter(jax.random.split(key, 48))
    f32 = jnp.float32
    nrm = lambda shape, scale: jax.random.normal(next(ks), shape, f32) * scale
    return {
        "x_prompt": nrm((BATCH, SEQ, D_MODEL), 1.0),
        "x_sample": nrm((DEC_BATCH, DEC_SEQ, D_MODEL), 1.0),
        "state_b_shift": nrm((N_AB, DEC_BATCH, D_B_SHIFT), 1.0),
        "state_b_wkv": nrm((N_AB, DEC_BATCH, H_B, HEAD_B, HEAD_B), 0.3),
        "cache_c_k": nrm((N_C, DEC_BATCH, PAST_LEN, H_C, 2, HEAD_C), 1.0),
        "cache_c_v": nrm((N_C, DEC_BATCH, PAST_LEN, H_C, 2 * HEAD_C), 1.0),
        "ab_w_in": nrm((N_AB, D_MODEL, P_AB), D_MODEL ** -0.5),
        "ab_a_ln_g": 1.0 + nrm((N_AB, D_A), 0.02),
        "ab_a_ln_b": nrm((N_AB, D_A), 0.02),
        "ab_a_ws": nrm((N_AB, G_A, MLP_CHUNK, MLP_CHUNK), MLP_CHUNK ** -0.5),
        "ab_a_bs": 1.0 + nrm((N_AB, G_A, MLP_CHUNK), 0.1),
        "ab_b_mu": jax.random.uniform(next(ks), (N_AB, D_B_SHIFT), f32),
        "ab_b_w0": -3.0 + nrm((N_AB, D_B), 1.0),
        "ab_b_w2": nrm((N_AB, R_W, D_B), 0.1 * R_W ** -0.5),
        "ab_b_a0": nrm((N_AB, D_B), 0.1),
        "ab_b_a2": nrm((N_AB, R_A, D_B), 0.1 * R_A ** -0.5),
        "ab_b_kk": 0.85 + nrm((N_AB, D_B), 0.02),
        "ab_b_ka": 1.0 + nrm((N_AB, D_B), 0.02),
        "ab_b_rk": nrm((N_AB, H_B, HEAD_B), 0.1),
        "ab_b_lnx_g": 1.0 + nrm((N_AB, D_B), 0.02),
        "ab_b_lnx_b": nrm((N_AB, D_B), 0.02),
        "ab_w_out": nrm((N_AB, D_A + D_B, D_MODEL), (D_A + D_B) ** -0.5 * DEEPNORM_BETA),
        "ab_ln_g": 1.0 + nrm((N_AB, D_MODEL), 0.02),
        "ab_ln_b": nrm((N_AB, D_MODEL), 0.02),
        "c_w_in": nrm((N_C, D_MODEL, P_C), D_MODEL ** -0.5),
        "c_lam_q1": nrm((N_C, HEAD_C), 0.1),
        "c_lam_k1": nrm((N_C, HEAD_C), 0.1),
        "c_lam_q2": nrm((N_C, HEAD_C), 0.1),
        "c_lam_k2": nrm((N_C, HEAD_C), 0.1),
        "c_subln_g": 1.0 + nrm((N_C, 2 * HEAD_C), 0.02),
        "c_w_out": nrm((N_C, D_C, D_MODEL), D_C ** -0.5 * DEEPNORM_BETA),
        "c_ln_g": 1.0 + nrm((N_C, D_MODEL), 0.02),
        "c_ln_b": nrm((N_C, D_MODEL), 0.02),
    }


def reference(x_prompt, x_sample, state_b_shift, state_b_wkv, cache_c_k, cache_c_v,
              ab_w_in, ab_a_ln_g, ab_a_ln_b, ab_a_ws, ab_a_bs, ab_b_mu, ab_b_w0, ab_b_w2,
              ab_b_a0, ab_b_a2, ab_b_kk, ab_b_ka, ab_b_rk, ab_b_lnx_g, ab_b_lnx_b, ab_w_out,
              ab_ln_g, ab_ln_b, c_w_in, c_lam_q1, c_lam_k1, c_lam_q2, c_lam_k2, c_subln_g,
              c_w_out, c_ln_g, c_ln_b):
    x_p, x_s = x_prompt, x_sample
    sh_p_l, wkv_p_l, sh_s_l, wkv_s_l, va_s_l = [], [], [], [], []
    kp_l, vp_l, ks_l, vs_l = [], [], [], []
    for li in range(DEPTH):
        j = li // 2
        if li % 2 == 0:
            prm = (ab_w_in[j], ab_a_ln_g[j], ab_a_ln_b[j], ab_a_ws[j], ab_a_bs[j], ab_b_mu[j],
                   ab_b_w0[j], ab_b_w2[j], ab_b_a0[j], ab_b_a2[j], ab_b_kk[j], ab_b_ka[j],
                   ab_b_rk[j], ab_b_lnx_g[j], ab_b_lnx_b[j], ab_w_out[j], ab_ln_g[j], ab_ln_b[j])
            shift0 = jnp.zeros((x_p.shape[0], D_B_SHIFT), x_p.dtype)
            wkv0 = jnp.zeros((x_p.shape[0], H_B, HEAD_B, HEAD_B), jnp.float32)
            x_p, sh_p, wkv_p, _ = ab_layer(x_p, shift0, wkv0, *prm)
            x_s, sh_s, wkv_s, va_s = ab_layer(x_s, state_b_shift[j], state_b_wkv[j], *prm)
            sh_p_l.append(sh_p.astype(state_b_shift.dtype))
            wkv_p_l.append(wkv_p.astype(state_b_wkv.dtype))
            sh_s_l.append(sh_s.astype(state_b_shift.dtype))
            wkv_s_l.append(wkv_s.astype(state_b_wkv.dtype))
            va_s_l.append(va_s)
        else:
            lam_init = 0.8 - 0.6 * math.exp(-0.3 * li)
            prm = (c_w_in[j], c_lam_q1[j], c_lam_k1[j], c_lam_q2[j], c_lam_k2[j], c_subln_g[j],
                   c_w_out[j], c_ln_g[j], c_ln_b[j])
            x_p, k_p, v_p = c_layer(x_p, None, None, lam_init, *prm)
            x_s, k_s, v_s = c_layer(x_s, cache_c_k[j], cache_c_v[j], lam_init, *prm)
            kp_l.append(k_p.astype(cache_c_k.dtype))
            vp_l.append(v_p.astype(cache_c_v.dtype))
            ks_l.append(k_s.astype(cache_c_k.dtype))
            vs_l.append(v_s.astype(cache_c_v.dtype))
    return (x_p, x_s, jnp.stack(sh_p_l), jnp.stack(wkv_p_l), jnp.stack(sh_s_l), jnp.stack(wkv_s_l),
            jnp.stack(va_s_l), jnp.stack(kp_l), jnp.stack(vp_l), jnp.stack(ks_l), jnp.stack(vs_l))
```

```python
from contextlib import ExitStack
import numpy as np
import concourse.bass as bass
import concourse.mybir as mybir
from concourse.bass_utils import run_bass_kernel_spmd

F32 = mybir.dt.float32
BF16 = mybir.dt.bfloat16
AF = mybir.ActivationFunctionType
ALU = mybir.AluOpType
AX = mybir.AxisListType
ENGS = ("pe", "act", "dve", "pool", "sp")
SEM_ROT = 30000

D = 2048
SEQ = 4096
NT = SEQ // 128
PAST = 4096
PAB = 7296
ALPHA = 4.0 ** 0.25
LAM_INIT = 0.8 - 0.6 * float(np.exp(-0.3))
C_DEC = float(np.exp(-0.5))


class Op:
    __slots__ = ("eng", "fn", "deps", "signaled", "token", "dma", "waits")

    def __init__(self, eng, fn, dma):
        self.eng = eng; self.fn = fn; self.deps = []; self.signaled = False
        self.token = None; self.dma = dma; self.waits = []


class Sched:
    def __init__(self, nc):
        self.nc = nc; self.ops = []; self.res = {}
        self.ndma = {"sp": 24, "pool": 12, "act": 4, "pe": 1, "dve": 1}
        self.bar_deps = []; self.need_bar = set()
        self.last = {}; self.dmas = []

    def barrier(self):
        self.bar_deps = list(self.last.values()) + self.dmas
        self.dmas = []; self.need_bar = set(ENGS)

    def op(self, eng, fn, reads=(), writes=(), dma=False):
        o = Op(eng, fn, dma); deps = {}

        def add(d):
            if d is None: return
            if (not d.dma) and d.eng == "pe" and eng == "pe" and not dma: return
            deps[id(d)] = d
        if eng in self.need_bar:
            self.need_bar.discard(eng)
            for d in self.bar_deps: deps[id(d)] = d
        for r in reads:
            st = self.res.get(r)
            if st is not None: add(st[0])
        for w in writes:
            st = self.res.get(w)
            if st is not None:
                add(st[0])
                for rd in st[1].values(): add(rd)
        o.deps = list(deps.values())
        for d in o.deps: d.signaled = True
        for r in reads:
            st = self.res.setdefault(r, [None, {}])
            st[1][("dma", id(o)) if dma else eng] = o
        for w in writes: self.res[w] = [o, {}]
        if dma:
            o.signaled = True; self.dmas.append(o)
        else:
            self.last[eng] = o
        self.ops.append(o)
        return o

    def pe(self, fn, reads=(), writes=()): return self.op("pe", fn, reads, writes)
    def act(self, fn, reads=(), writes=()): return self.op("act", fn, reads, writes)
    def dve(self, fn, reads=(), writes=()): return self.op("dve", fn, reads, writes)
    def pool(self, fn, reads=(), writes=()): return self.op("pool", fn, reads, writes)

    def dma(self, out, in_, reads=(), writes=(), q="sp", **kw):
        return self.op(q, lambda e: e.dma_start(out=out, in_=in_, **kw), reads, writes, dma=True)

    def finalize(self):
        cnt = {e: 0 for e in ENGS}; rr = {e: 0 for e in ENGS}
        dcnt = {}; dlast = {}; per = {e: [] for e in ENGS}; waited = {e: {} for e in ENGS}; keys = set()
        for o in self.ops:
            e = o.eng; w = waited[e]; need = {}
            for d in o.deps:
                k, v = d.token
                if w.get(k, 0) < v and need.get(k, 0) < v: need[k] = v
            if o.dma:
                n = self.ndma[e]; slot = rr[e] % n; rr[e] += 1
                c = dcnt.get((e, slot), 0); gen, idx = divmod(c, 1800)
                k = ("d", e, slot, gen); prev = dlast.get((e, slot))
                if prev is not None:
                    pk, pv = prev
                    if w.get(pk, 0) < pv and need.get(pk, 0) < pv: need[pk] = pv
                o.token = (k, 16 * (idx + 1)); dcnt[(e, slot)] = c + 1; dlast[(e, slot)] = o.token; keys.add(k)
            elif o.signaled:
                c = cnt[e]; gen, idx = divmod(c, SEM_ROT); k = ("c", e, gen)
                o.token = (k, idx + 1); cnt[e] = c + 1; keys.add(k)
            for k, v in need.items():
                w[k] = v; o.waits.append((k, v))
            per[e].append(o)
        self.per = per; self.keys = sorted(keys, key=str); self.finals = list(dlast.values())
        for e in ENGS:
            if cnt[e]:
                gen, idx = divmod(cnt[e] - 1, SEM_ROT); self.finals.append((("c", e, gen), idx + 1))

    def emit(self):
        import os
        mo = int(os.environ.get('K_MAXOPS', '0'))
        if mo: self.ops = self.ops[:mo]
        nc = self.nc; self.finalize()
        with ExitStack() as st:
            sems = {k: st.enter_context(nc.semaphore("s%d" % i)) for i, k in enumerate(self.keys)}
            block = st.enter_context(nc.Block()); per = self.per; finals = self.finals

            def run(en, e, last=False):
                for o in per[en]:
                    for (k, v) in o.waits: e.wait_ge(sems[k], v)
                    ins = o.fn(e)
                    if o.token is not None: ins.then_inc(sems[o.token[0]], 16 if o.dma else 1)
                if last:
                    for (k, v) in finals: e.wait_ge(sems[k], v)

            @block.tensor
            def _(e): run("pe", e)

            @block.scalar
            def _(e): run("act", e)

            @block.vector
            def _(e): run("dve", e)

            @block.gpsimd
            def _(e): run("pool", e)

            @block.sync
            def _(e): run("sp", e, last=True)
        return {e: len(per[e]) for e in ENGS}, len(self.keys)


def build(do_l1=True):
    nc = bass.Bass("TRN2", target_bir_lowering=False)
    S = Sched(nc)
    din = lambda n, s, dt=F32: nc.dram_tensor(n, list(s), dt, kind="ExternalInput").ap()
    dout = lambda n, s, dt=F32: nc.dram_tensor(n, list(s), dt, kind="ExternalOutput").ap()
    dscr = lambda n, s, dt: nc.dram_tensor(n, list(s), dt).ap()
    xT_p = din("xT_p", [D, SEQ]); x_p = din("x_p", [SEQ, D])
    xT_s = din("xT_s", [D, 64]); x_s = din("x_s", [64, D])
    st_shift = din("st_shift", [3200]); st_wkvT = din("st_wkvT", [16, 64, 64])
    if do_l1:
        kcT = din("kcT", [16, 128, PAST]); vc = din("vc", [PAST, 2048])
    w_in = din("w_in", [D, PAB]); w_out = din("w_out", [D, D]); cw_in = din("cw_in", [D, 8192]); cw_out = din("cw_out", [D, D])
    a_ln_g = din("a_ln_g", [1, 1024]); a_ln_b = din("a_ln_b", [1, 1024])
    a_wsT = din("a_wsT", [4, 128, 128]); a_bs = din("a_bs", [1, 512])
    b_mu = din("b_mu", [3200]); b_w0 = din("b_w0", [1024]); b_w2 = din("b_w2", [64, 1024]); b_a0 = din("b_a0", [1024])
    b_a2 = din("b_a2", [64, 1024]); b_kk = din("b_kk", [1024]); b_ka = din("b_ka", [1024]); b_rk = din("b_rk", [1024])
    b_lnx_g = din("b_lnx_g", [1, 1024]); b_lnx_b = din("b_lnx_b", [1, 1024])
    ln_g = din("ln_g", [1, D]); ln_b = din("ln_b", [1, D]); cln_g = din("cln_g", [1, D]); cln_b = din("cln_b", [1, D])
    lamv = din("lamv", [1, 512]); subln_g = din("subln_g", [1, 256])
    mask2 = din("mask2", [128, 256]); sel_in = din("sel", [128, 2])
    y_p = dout("y_p", [SEQ if not do_l1 else SEQ // 2, D]); y_s = dout("y_s", [64, D])
    sh_p = dout("sh_p", [3200]); wkvT_p = dout("wkvT_p", [16, 64, 64]); sh_s = dout("sh_s", [3200]); wkvT_s = dout("wkvT_s", [16, 64, 64])
    va_s = dout("va_s", [64, 1024])
    k_p = dout("k_p", [SEQ, D]); v_p = dout("v_p", [SEQ, D]); k_s = dout("k_s", [64, D]); v_s = dout("v_s", [64, D])
    Wb_in = dscr("Wb_in", [29, 128, 16, 256], BF16); Wb_out = dscr("Wb_out", [8, 128, 16, 256], BF16)
    Wc_in = dscr("Wc_in", [32, 128, 16, 256], BF16); Wc_out = dscr("Wc_out", [8, 128, 16, 256], BF16)
    x1_all = dscr("x1_all", [SEQ + 64, D], F32); x1T_all = dscr("x1T_all", [D, SEQ + 64], BF16)
    KT = dscr("KT", [16, 128, SEQ + 64], BF16); Vb = dscr("Vb", [SEQ + 64, D], BF16)


    with ExitStack() as st:
        sb = lambda n, s, d=F32: st.enter_context(nc.sbuf_tensor(n, list(s), d))
        psb = lambda n, s, d=F32: st.enter_context(nc.psum_tensor(n, list(s), d))
        for (src, dst, nm) in ((w_in, Wb_in, "Wb_in"), (w_out, Wb_out, "Wb_out"), (cw_in, Wc_in, "Wc_in"), (cw_out, Wc_out, "Wc_out")):
            nbk = src.shape[1] // 256
            for a in range(16):
                S.dma(dst[0:nbk, :, a, :].rearrange("b p c -> p b c"), src[a * 128:(a + 1) * 128, 0:nbk * 256].rearrange("p (b c) -> p b c", c=256), writes=[(nm, a)], q="pool")
                if src.shape[1] % 256:
                    S.dma(dst[nbk, :, a, 0:128], src[a * 128:(a + 1) * 128, nbk * 256:nbk * 256 + 128], writes=[(nm, a, "t")], q="pool")
        WRES = lambda nm: [(nm, a) for a in range(16)] + ([(nm, a, "t") for a in range(16)] if nm == "Wb_in" else [])
        idf = sb("idf", [128, 128]); idb = sb("idb", [128, 128], BF16)
        mT2 = sb("mT2", [128, 128]); mL = sb("mL", [128, 64]); tri = sb("tri", [128, 128]); bones = sb("bones", [128, 128])
        hind = sb("hind", [128, 2])
        S.pool(lambda e: e.memset(idf[:], 1.0), writes=["idf"])
        S.pool(lambda e: e.affine_select(out=idf[:], in_=idf[:], pattern=[[-1, 128]], compare_op=ALU.is_equal, fill=0.0, base=0, channel_multiplier=1), reads=["idf"], writes=["idf"])
        S.dve(lambda e: e.tensor_copy(out=idb[:], in_=idf[:]), reads=["idf"], writes=["idb"])
        S.pool(lambda e: e.memset(mT2[:], 1.0), writes=["mT2"])
        S.pool(lambda e: e.memset(mL[:], 1.0), writes=["mL"])
        for hf in range(1):
            r0 = hf * 64
            S.pool(lambda e, r0=r0: e.affine_select(out=mT2[r0:r0 + 64, 0:64], in_=mT2[r0:r0 + 64, 0:64], pattern=[[1, 64]], compare_op=ALU.is_gt, fill=0.0, base=r0, channel_multiplier=-1), reads=["mT2"], writes=["mT2"])
            S.pool(lambda e, r0=r0: e.affine_select(out=mT2[r0:r0 + 64, 64:128], in_=mT2[r0:r0 + 64, 64:128], pattern=[[1, 64]], compare_op=ALU.is_ge, fill=0.0, base=r0, channel_multiplier=-1), reads=["mT2"], writes=["mT2"])
            S.pool(lambda e, r0=r0: e.affine_select(out=mL[r0:r0 + 64, :], in_=mL[r0:r0 + 64, :], pattern=[[-1, 64]], compare_op=ALU.is_gt, fill=0.0, base=-r0, channel_multiplier=1), reads=["mL"], writes=["mL"])
        S.dma(mT2[64:128, :], mT2[0:64, :], reads=["mT2"], writes=["mT2"])
        S.dma(mL[64:128, :], mL[0:64, :], reads=["mL"], writes=["mL"])
        S.pool(lambda e: e.memset(tri[:], 1.0), writes=["tri"])
        S.pool(lambda e: e.affine_select(out=tri[:], in_=tri[:], pattern=[[1, 128]], compare_op=ALU.is_ge, fill=0.0, base=0, channel_multiplier=-1), reads=["tri"], writes=["tri"])
        S.pool(lambda e: e.memset(tri[0:64, 64:128], 0.0), reads=["tri"], writes=["tri"])
        S.pool(lambda e: e.memset(bones[:], 0.0), writes=["bones"])
        S.pool(lambda e: e.memset(bones[0:64, 0:64], 1.0), reads=["bones"], writes=["bones"])
        S.pool(lambda e: e.memset(bones[64:128, 64:128], 1.0), reads=["bones"], writes=["bones"])
        S.pool(lambda e: e.memset(hind[:], 0.0), writes=["hind"])
        S.pool(lambda e: e.memset(hind[0:64, 0:1], 1.0), reads=["hind"], writes=["hind"])
        S.pool(lambda e: e.memset(hind[64:128, 1:2], 1.0), reads=["hind"], writes=["hind"])
        gb0 = sb("gb0", [128, D]); gb1 = sb("gb1", [128, D])
        wsT = gb0[:, 0:512].rearrange("p (g t) -> p g t", t=128); wsTb = sb("wsTb", [128, 4, 128], BF16); bias8 = sb("bias8", [128, 8, 128])
        S.dma(wsT, a_wsT.rearrange("g s t -> s g t"), writes=["gb0"])
        S.pool(lambda e: e.affine_select(out=wsT, in_=wsT, pattern=[[0, 4], [1, 128]], compare_op=ALU.is_ge, fill=0.0, base=0, channel_multiplier=-1), reads=["gb0"], writes=["gb0"])
        S.dve(lambda e: e.tensor_copy(out=wsTb[:], in_=wsT), reads=["gb0"], writes=["wsTb"])
        for g in range(4):
            for r in range(2):
                S.dma(bias8[:, 2 * g + r, :], a_bs[:, g * 128:(g + 1) * 128].partition_broadcast(128), writes=[("bias8", 2 * g + r)])
        BIAS8 = [("bias8", i) for i in range(8)]
        biasT = sb("biasT", [128, 4])
        S.dma(biasT[:], a_bs.rearrange("o (g t) -> t (o g)", g=4), writes=["biasT"], allow_slow_non_contiguous=True)
        def load_gb(gsrc, bsrc, w):
            S.dma(gb0[:, 0:w], gsrc.partition_broadcast(128), writes=["gb0"])
            S.dma(gb1[:, 0:w], bsrc.partition_broadcast(128), writes=["gb1"])
        alg = lxg = lng = gb0; alb = lxb = lnb = gb1
        mu25 = sb("mu25", [128, 25]); cols = sb("cols", [128, 6, 8]); w2a2 = sb("w2a2", [128, 1024])
        S.dma(mu25[:], b_mu.rearrange("(a p) -> p a", p=128), writes=["mu25"], allow_slow_non_contiguous=True)
        for i, v_ in enumerate((b_w0, b_a0, b_kk, b_ka, b_rk)):
            S.dma(cols[:, i, :], v_.rearrange("(a p) -> p a", p=128), writes=[("cols", i)], allow_slow_non_contiguous=True)
        S.dve(lambda e: e.tensor_scalar(out=cols[:, 5, :], in0=cols[:, 3, :], scalar1=-1.0, scalar2=1.0, op0=ALU.mult, op1=ALU.add), reads=[("cols", 3)], writes=[("cols", 5)])
        COLS = [("cols", i) for i in range(6)]
        S.dma(w2a2[0:64, :], b_w2, writes=["w2a2a"]); S.dma(w2a2[64:128, :], b_a2, writes=["w2a2b"])
        xT = [sb("xT%d" % i, [128, 16, 128], BF16) for i in range(1)]
        wb = [sb("wb%d" % i, [128, 16, 256], BF16) for i in range(3)]
        wb_rr = [0]
        ps = [psb("ps%d" % i, [128, 512]) for i in range(6)]
        psT = psb("psT", [128, 512]); psTb = psb("psTb", [128, 1024], BF16)
        ps_rr = [0]
        big = [sb("big%d" % i, [128, 2048]) for i in range(2)]
        oT = sb("oT", [128, 16, 128], BF16)
        vn = sb("vn", [128, 1024]); vnb = sb("vnb", [128, 1024], BF16)
        sgb = sb("sgb", [128, 8, 128]); sil = sb("sil", [128, 128]); tmpa = sb("tmpa", [128, 128])
        st8 = sb("st8", [128, 16])
        hcur2 = sb("hcur2", [128, 3, 129]); hs2 = sb("hs2", [128, 3, 128]); sig2 = sb("sig2", [128, 128]); a_t2 = sb("a_t2", [128, 128]); kkn2 = sb("kkn2", [128, 128]); tb2 = [sb("tbb%d" % i, [128, 128]) for i in range(4)]
        hcur = sb("hcur", [128, 3, 129]); hs = sb("hs", [128, 3, 128]); lora = sb("lora", [128, 128]); h24 = sb("h24", [128, 129])
        shl = [sb("shl%d" % i, [128, 25]) for i in range(2)]
        sig = sb("sig", [128, 128]); a_t = sb("a_t", [128, 128]); kkn = sb("kkn", [128, 128]); tb = [sb("tb%d" % i, [128, 128]) for i in range(4)]
        winc = sb("winc", [128, 8, 128]); ar = sb("ar", [128, 8, 2, 128]); bT = sb("bT", [128, 8, 128]); kTt = sb("kTt", [128, 8, 128])
        btok = sb("btok", [128, 1024]); ktok = sb("ktok", [128, 1024]); vtok = sb("vtok", [128, 1024]); ytok = sb("ytok", [128, 1024]); bv = sb("bv", [128, 1024])
        bsum = sb("bsum", [128, 16]); St = [sb("St%d" % i, [128, 8, 64]) for i in range(2)]
        AB4 = sb("AB4", [128, 4, 4, 128]); AK4 = sb("AK4", [128, 4, 4, 128]); Np = sb("Np", [128, 4, 2, 4, 64]); NTp = sb("NTp", [128, 4, 6, 4, 64])
        X4 = sb("X4", [128, 4, 4, 64]); t1s = sb("t1s", [128, 64])
        x1b = sb("x1b", [128, D], BF16); x1T = sb("x1T", [128, 16, 128], BF16)

        MARKS = []
        DBGS = {}
        def DBG(name, ap, reads):
            import os
            if not os.environ.get("K_DBG") or name in DBGS: return
            shp = list(ap.shape)
            t_ = nc.dram_tensor("dbg_" + name, shp, F32, kind="ExternalOutput").ap()
            DBGS[name] = t_
            S.dma(t_, ap, reads=reads)
        def MARK(n):
            MARKS.append((n, len(S.ops)))
        def wload(W, nm, c0, n):
            s_ = wb_rr[0] % len(wb); wb_rr[0] += 1
            assert c0 % 256 == 0
            S.dma(wb[s_][:, :, :], W[c0 // 256], reads=WRES(nm), writes=[("wb", s_)])
            return s_

        def nps():
            i = ps_rr[0] % len(ps); ps_rr[0] += 1
            return i

        def mm(out, lhsT, rhs, start, stop, reads, writes, tp=None):
            if tp is None:
                S.pe(lambda e: e.matmul(out, lhsT=lhsT, rhs=rhs, start=start, stop=stop), reads, writes)
            else:
                S.pe(lambda e: e.matmul(out, lhsT=lhsT, rhs=rhs, start=start, stop=stop, tile_position=tp), reads, writes)

        def ln_rows(src_ap, width, T, dst_ap, gt, bt_, eps, rsrc, rdst, gn, bn):
            junk = x1b
            S.dve(lambda e: e.memset(st8[0:T, 0:2], 0.0), writes=["st8a", "st8b"])
            S.act(lambda e: e.activation(out=junk[0:T, 0:width], in_=src_ap, func=AF.Identity, accum_out=st8[0:T, 0:1]), reads=rsrc + ["st8a"], writes=["x1b", "st8a"])
            S.act(lambda e: e.activation(out=junk[0:T, 0:width], in_=src_ap, func=AF.Square, accum_out=st8[0:T, 1:2]), reads=rsrc + ["x1b", "st8b"], writes=["x1b", "st8b"])
            S.dve(lambda e: e.tensor_scalar(out=st8[0:T, 2:3], in0=st8[0:T, 0:1], scalar1=1.0 / width, scalar2=None, op0=ALU.mult), reads=["st8a"], writes=["st8c"])
            S.dve(lambda e: e.tensor_tensor(out=st8[0:T, 3:4], in0=st8[0:T, 2:3], in1=st8[0:T, 2:3], op=ALU.mult), reads=["st8c"], writes=["st8d"])
            S.dve(lambda e: e.scalar_tensor_tensor(out=st8[0:T, 4:5], in0=st8[0:T, 1:2], scalar=1.0 / width, in1=st8[0:T, 3:4], op0=ALU.mult, op1=ALU.subtract), reads=["st8b", "st8d"], writes=["st8e"])
            S.dve(lambda e: e.tensor_scalar(out=st8[0:T, 4:5], in0=st8[0:T, 4:5], scalar1=eps, scalar2=None, op0=ALU.add), reads=["st8e"], writes=["st8e"])
            S.act(lambda e: e.activation(out=st8[0:T, 5:6], in_=st8[0:T, 4:5], func=AF.Sqrt), reads=["st8e"], writes=["st8f"])
            S.dve(lambda e: e.reciprocal(out=st8[0:T, 6:7], in_=st8[0:T, 5:6]), reads=["st8f"], writes=["st8g"])
            S.dve(lambda e: e.scalar_tensor_tensor(out=st8[0:T, 7:8], in0=st8[0:T, 2:3], scalar=-1.0, in1=st8[0:T, 6:7], op0=ALU.mult, op1=ALU.mult), reads=["st8c", "st8g"], writes=["st8h"])
            S.act(lambda e: e.activation(out=dst_ap, in_=src_ap, func=AF.Identity, bias=st8[0:T, 7:8], scale=st8[0:T, 6:7]), reads=rsrc + ["st8g", "st8h"], writes=rdst)
            S.dve(lambda e: e.tensor_tensor(out=dst_ap, in0=dst_ap, in1=gt[0:T, 0:width], op=ALU.mult), reads=rdst + ["gb0"], writes=rdst)
            S.dve(lambda e: e.tensor_tensor(out=dst_ap, in0=dst_ap, in1=bt_[0:T, 0:width], op=ALU.add), reads=rdst + ["gb1"], writes=rdst)

        def l0_tile(T, xT_src, x_src, xs, sti, first, last, outs, row0):
            xt = xT[xs]; XR = ("xT", xs)
            S.dma(xt[:, :, 0:T], xT_src.rearrange("(a p) t -> p a t", p=128), writes=[XR], q="pool")
            nch = T // 64
            STR = ("St", sti); SHR = ("shl", sti)
            stt = St[sti]; shlast = shl[sti]

            def lin_cm(pi, s_, c0, reads_w):
                for a in range(16):
                    mm(ps[pi][:, 0:T], wb[s_][:, a, c0:c0 + 128], xt[:, a, 0:T], a == 0, a == 15, [XR, ("wb", s_)], [("ps", pi)])

            load_gb(a_ln_g, a_ln_b, 1024)
            pa, pb = nps(), nps()
            for q4 in range(4):
                s_ = wload(Wb_in, "Wb_in", 1024 + q4 * 256, 256); pi = (pa, pb)[q4 // 2]
                for a in range(16):
                    mm(ps[pi][0:T, (q4 % 2) * 256:(q4 % 2) * 256 + 256], xt[:, a, 0:T], wb[s_][:, a, :], a == 0, a == 15, [XR, ("wb", s_)], [("ps", pi)])
            S.act(lambda e: e.copy(out=big[0][0:T, 0:512], in_=ps[pa][0:T, :]), reads=[("ps", pa)], writes=["big0a"])
            S.dve(lambda e: e.tensor_copy(out=big[0][0:T, 512:1024], in_=ps[pb][0:T, :]), reads=[("ps", pb)], writes=["big0b"])
            ln_rows(big[0][0:T, 0:1024], 1024, T, vn[0:T, :], alg, alb, 1e-5, ["big0a", "big0b"], ["vn"], "alg", "alb")
            S.act(lambda e: e.copy(out=vnb[0:T, :], in_=vn[0:T, :]), reads=["vn"], writes=["vnb"])
            if outs.get("va") is not None:
                S.dma(outs["va"], vn[0:T, :], reads=["vn"])
            sgt = sgb[:].rearrange("p a t -> p (a t)")
            psg = [nps(), nps()]
            for g in range(4):
                pi = psg[g // 2]; cs_ = slice((g % 2) * 256, (g % 2) * 256 + 256)
                mm(ps[pi][0:T, cs_], wsTb[0:T, g, 0:T], vnb[0:T, g * 256:(g + 1) * 256], True, True, ["vnb", "wsTb"], [("ps", pi)])
            for g in range(4):
                pi = psg[g // 2]; cs_ = slice((g % 2) * 256, (g % 2) * 256 + 256)
                S.dve(lambda e, pi=pi, cs_=cs_, g=g: e.tensor_scalar(out=sgt[0:T, g * 256:(g + 1) * 256], in0=ps[pi][0:T, cs_], scalar1=biasT[0:T, g:g + 1], scalar2=None, op0=ALU.add), reads=[("ps", pi), "biasT"], writes=[("sgb", g)])
            for blk in range(4):
                su = wload(Wb_in, "Wb_in", blk * 256, 256); sg_ = wload(Wb_in, "Wb_in", 2048 + blk * 256, 256)
                pu, pg = nps(), nps()
                for (pi, s_) in ((pu, su), (pg, sg_)):
                    for a in range(16):
                        mm(ps[pi][0:T, 0:256], xt[:, a, 0:T], wb[s_][:, a, :], a == 0, a == 15, [XR, ("wb", s_)], [("ps", pi)])
                bs_ = slice(blk * 256, (blk + 1) * 256)
                S.act(lambda e, pg=pg, bs_=bs_: e.activation(out=big[0][0:T, bs_], in_=ps[pg][0:T, 0:256], func=AF.Silu), reads=[("ps", pg), "big0a", "big0b"], writes=["big0a", "big0b"])
                S.dve(lambda e, pu=pu, bs_=bs_: e.tensor_tensor(out=big[1][0:T, bs_], in0=ps[pu][0:T, 0:256], in1=sgt[0:T, bs_], op=ALU.mult), reads=[("ps", pu), ("sgb", blk), "big1"], writes=["big1"])
                S.dve(lambda e, bs_=bs_: e.tensor_tensor(out=x1b[0:T, bs_], in0=big[1][0:T, bs_], in1=big[0][0:T, bs_], op=ALU.mult), reads=["big1", "big0a", "big0b", "x1b"], writes=["x1b"])
            for cc in range(8):
                S.pe(lambda e, cc=cc: e.transpose(psTb[:, cc * 128:cc * 128 + T], x1b[0:T, cc * 128:(cc + 1) * 128], idb[0:T, 0:T]), reads=["x1b", "idb"], writes=["psTb"])
            S.act(lambda e: e.copy(out=oT[:, 0:8, 0:T], in_=psTb[:].rearrange("p (a t) -> p a t", t=128)[:, :, 0:T]), reads=["psTb"], writes=[("oT", i) for i in range(8)])

            s24 = wload(Wb_in, "Wb_in", 7168, 128)
            p24 = nps(); lin_cm(p24, s24, 0, None)
            if first:
                S.dma(shlast[:], outs["shift0"].rearrange("(a p) -> p a", p=128), writes=[SHR], allow_slow_non_contiguous=True) if outs.get("shift0") is not None else \
                    S.pool(lambda e: e.memset(shlast[:], 0.0), writes=[SHR])
            S.act(lambda e: e.copy(out=h24[:, 1:T + 1], in_=ps[p24][:, 0:T]), reads=[("ps", p24)], writes=["h24"])
            S.dve(lambda e: e.tensor_copy(out=h24[:, 0:1], in_=shlast[:, 24:25]), reads=[SHR], writes=["h24p"])
            S.dve(lambda e: e.tensor_tensor(out=tb[0][:, 0:T], in0=h24[:, 0:T], in1=h24[:, 1:T + 1], op=ALU.subtract), reads=["h24", "h24p"], writes=["tb0"])
            S.dve(lambda e: e.scalar_tensor_tensor(out=lora[:, 0:T], in0=tb[0][:, 0:T], scalar=mu25[:, 24:25], in1=h24[:, 1:T + 1], op0=ALU.mult, op1=ALU.add), reads=["tb0", "h24", "mu25"], writes=["lora"])
            S.dve(lambda e: e.tensor_copy(out=shlast[:, 24:25], in_=h24[:, T:T + 1]), reads=["h24", "h24p", SHR], writes=[SHR])
            S.act(lambda e: e.activation(out=lora[0:64, 0:T], in_=lora[0:64, 0:T], func=AF.Tanh), reads=["lora"], writes=["lora"])
            hcurs = [hcur, hcur2]; hss = [hs, hs2]; sigs = [sig, sig2]; a_ts = [a_t, a_t2]; kkns = [kkn, kkn2]; tbs = [tb, tb2]

            def prep(hp, sl):
                par = hp % 2
                hcur_ = hcurs[par]; hs_t = hss[par]; sig_ = sigs[par]; a_t_ = a_ts[par]; kkn_ = kkns[par]; tb_ = tbs[par]
                pis = [nps() for _ in range(3)]
                for j in range(3):
                    lin_cm(pis[j], sl[j], (hp % 2) * 128, None)
                    chn = j * 8 + hp
                    S.act(lambda e, j=j, pi=pis[j]: e.copy(out=hcur_[:, j, 1:T + 1], in_=ps[pi][:, 0:T]), reads=[("ps", pis[j])], writes=[("hcur", par, j)])
                    S.dve(lambda e, j=j, chn=chn: e.tensor_copy(out=hcur_[:, j, 0:1], in_=shlast[:, chn:chn + 1]), reads=[SHR], writes=[("hcurp", par, j)])
                    S.dve(lambda e, j=j: e.tensor_tensor(out=tb_[0][:, 0:T], in0=hcur_[:, j, 0:T], in1=hcur_[:, j, 1:T + 1], op=ALU.subtract), reads=[("hcur", par, j), ("hcurp", par, j)], writes=[("tb0", par)])
                    S.dve(lambda e, j=j, chn=chn: e.scalar_tensor_tensor(out=hs_t[:, j, 0:T], in0=tb_[0][:, 0:T], scalar=mu25[:, chn:chn + 1], in1=hcur_[:, j, 1:T + 1], op0=ALU.mult, op1=ALU.add),
                          reads=[("tb0", par), ("hcur", par, j), "mu25"], writes=[("hs", par, j)])
                    S.dve(lambda e, j=j, chn=chn: e.tensor_copy(out=shlast[:, chn:chn + 1], in_=hcur_[:, j, T:T + 1]), reads=[("hcur", par, j), ("hcurp", par, j), SHR], writes=[SHR])
                yield
                rs_, ks_, vs_ = hs_t[:, 0, 0:T], hs_t[:, 1, 0:T], hs_t[:, 2, 0:T]
                pz = nps()
                mm(ps[pz][:, 0:T], w2a2[0:64, hp * 128:(hp + 1) * 128], lora[0:64, 0:T], True, True, ["w2a2a", "lora"], [("ps", pz)])
                S.act(lambda e, pz=pz, hp=hp: e.activation(out=sig_[:, 0:T], in_=ps[pz][:, 0:T], func=AF.Sigmoid, bias=cols[:, 0, hp:hp + 1], scale=1.0), reads=[("ps", pz)] + COLS, writes=[("sig", par)])
                pq = nps()
                mm(ps[pq][:, 0:T], w2a2[64:128, hp * 128:(hp + 1) * 128], lora[64:128, 0:T], True, True, ["w2a2b", "lora"], [("ps", pq)], tp=(64, 0))
                S.act(lambda e, pq=pq, hp=hp: e.activation(out=a_t_[:, 0:T], in_=ps[pq][:, 0:T], func=AF.Sigmoid, bias=cols[:, 1, hp:hp + 1], scale=1.0), reads=[("ps", pq)] + COLS, writes=[("a_t", par)])
                yield
                S.dve(lambda e, hp=hp: e.tensor_scalar(out=tb_[1][:, 0:T], in0=ks_, scalar1=cols[:, 2, hp:hp + 1], scalar2=None, op0=ALU.mult), reads=[("hs", par, 1)] + COLS, writes=[("tb1", par)])
                S.dve(lambda e: e.tensor_tensor(out=tb_[2][:, 0:T], in0=tb_[1][:, 0:T], in1=tb_[1][:, 0:T], op=ALU.mult), reads=[("tb1", par)], writes=[("tb2", par)])
                pn = nps()
                mm(ps[pn][:, 0:T], bones[:], tb_[2][:, 0:T], True, True, ["bones", ("tb2", par)], [("ps", pn)])
                yield
                S.act(lambda e, pn=pn: e.activation(out=tb_[2][:, 0:T], in_=ps[pn][:, 0:T], func=AF.Sqrt), reads=[("ps", pn)], writes=[("tb2", par)])
                S.dve(lambda e: e.tensor_scalar(out=tb_[2][:, 0:T], in0=tb_[2][:, 0:T], scalar1=1e-12, scalar2=None, op0=ALU.max), reads=[("tb2", par)], writes=[("tb2", par)])
                S.dve(lambda e: e.reciprocal(out=tb_[2][:, 0:T], in_=tb_[2][:, 0:T]), reads=[("tb2", par)], writes=[("tb2", par)])
                S.dve(lambda e: e.tensor_tensor(out=kkn_[:, 0:T], in0=tb_[1][:, 0:T], in1=tb_[2][:, 0:T], op=ALU.mult), reads=[("tb1", par), ("tb2", par)], writes=[("kkn", par)])
                yield
                S.dve(lambda e, hp=hp: e.tensor_scalar(out=tb_[1][:, 0:T], in0=a_t_[:, 0:T], scalar1=cols[:, 3, hp:hp + 1], scalar2=cols[:, 5, hp:hp + 1], op0=ALU.mult, op1=ALU.add), reads=[("a_t", par), ("tb1", par)] + COLS, writes=[("tb1", par)])
                S.dve(lambda e: e.tensor_tensor(out=tb_[1][:, 0:T], in0=tb_[1][:, 0:T], in1=ks_, op=ALU.mult), reads=[("tb1", par), ("hs", par, 1)], writes=[("tb1", par)])
                S.dve(lambda e: e.tensor_tensor(out=tb_[2][:, 0:T], in0=kkn_[:, 0:T], in1=a_t_[:, 0:T], op=ALU.mult), reads=[("kkn", par), ("a_t", par)], writes=[("tb2", par)])
                S.dve(lambda e, hp=hp: e.scalar_tensor_tensor(out=tb_[3][:, 0:T], in0=rs_, scalar=cols[:, 4, hp:hp + 1], in1=tb_[1][:, 0:T], op0=ALU.mult, op1=ALU.mult), reads=[("hs", par, 0), ("tb1", par)] + COLS, writes=[("tb3", par)])
                mm(psT[0:T, 256 + 2 * hp:256 + 2 * hp + 2], tb_[3][:, 0:T], hind[:], True, True, [("tb3", par), "hind"], ["psTbn"])
                yield
                S.pe(lambda e: e.transpose(psT[0:T, 0:128], sig_[:, 0:T], idf[:]), reads=[("sig", par), "idf"], writes=["psT"])
                S.act(lambda e: e.activation(out=tb_[3][0:T, :], in_=psT[0:T, 0:128], func=AF.Copy, scale=-C_DEC), reads=["psT", ("tb3", par)], writes=[("tb3", par)])
                pc = nps()
                mm(ps[pc][:, 0:T], tb_[3][0:T, :], tri[0:T, 0:T], True, True, [("tb3", par), "tri"], [("ps", pc)])
                yield
                S.act(lambda e, pc=pc, hp=hp: e.activation(out=winc[:, hp, 0:T], in_=ps[pc][:, 0:T], func=AF.Exp), reads=[("ps", pc)], writes=[("winc", hp)])
                S.act(lambda e, pc=pc: e.activation(out=tb_[3][:, 0:T], in_=ps[pc][:, 0:T], func=AF.Exp, scale=-1.0), reads=[("ps", pc), ("tb3", par)], writes=[("tb3", par)])
                S.dve(lambda e, pc=pc: e.scalar_tensor_tensor(out=sig_[:, 0:T], in0=sig_[:, 0:T], scalar=C_DEC, in1=ps[pc][:, 0:T], op0=ALU.mult, op1=ALU.add), reads=[("sig", par), ("ps", pc), "psT"], writes=[("sig", par)])
                S.act(lambda e: e.activation(out=sig_[:, 0:T], in_=sig_[:, 0:T], func=AF.Exp), reads=[("sig", par)], writes=[("sig", par)])
                yield
                arv = ar[:, hp].rearrange("p c (w t) -> p c w t", w=2)
                T3 = lambda ap: ap.rearrange("p (c t) -> p c t", t=64)
                S.dve(lambda e, hp=hp, arv=arv: e.tensor_tensor(out=arv[:, 0:nch, 1, :], in0=T3(rs_), in1=T3(winc[:, hp, 0:T]), op=ALU.mult), reads=[("hs", par, 0), ("winc", hp)], writes=[("ar", hp, 1)])
                S.dve(lambda e, hp=hp, arv=arv: e.scalar_tensor_tensor(out=arv[:, 0:nch, 0, :], in0=T3(kkn_[:, 0:T]), scalar=-1.0, in1=T3(sig_[:, 0:T]), op0=ALU.mult, op1=ALU.mult), reads=[("kkn", par), ("sig", par)], writes=[("ar", hp, 0)])
                S.dve(lambda e, hp=hp: e.tensor_tensor(out=bT[:, hp, 0:T], in0=tb_[2][:, 0:T], in1=tb_[3][:, 0:T], op=ALU.mult), reads=[("tb2", par), ("tb3", par)], writes=[("bT", hp)])
                S.dve(lambda e, hp=hp: e.tensor_tensor(out=kTt[:, hp, 0:T], in0=tb_[1][:, 0:T], in1=tb_[3][:, 0:T], op=ALU.mult), reads=[("tb1", par), ("tb3", par)], writes=[("kT", hp)])
                yield
                for (src_ap, dstt, rn, wn) in ((bT[:, hp, 0:T], btok, ("bT", hp), "btok"), (kTt[:, hp, 0:T], ktok, ("kT", hp), "ktok"), (vs_, vtok, ("hs", par, 2), "vtok")):
                    S.pe(lambda e, src_ap=src_ap: e.transpose(psT[0:T, 0:128], src_ap, idf[:]), reads=[rn, "idf"], writes=["psT"])
                    S.act(lambda e, dstt=dstt, hp=hp: e.copy(out=dstt[0:T, hp * 128:(hp + 1) * 128], in_=psT[0:T, 0:128]), reads=["psT"], writes=[(wn, hp)])
            for hp0 in range(0, 8, 2):
                sl_ = [wload(Wb_in, "Wb_in", 3072 + j * 1024 + (hp0 // 2) * 256, 256) for j in range(3)]
                gens = [prep(hp0, sl_), prep(hp0 + 1, sl_)]
                while gens:
                    for g_ in list(gens):
                        try:
                            next(g_)
                        except StopIteration:
                            gens.remove(g_)
            S.act(lambda e: e.copy(out=bsum[0:T, :], in_=psT[0:T, 256:272]), reads=["psTbn"], writes=["bsum"])
            V3 = lambda ap: ap.rearrange("p (h v) -> p h v", v=64)
            S.pool(lambda e: e.tensor_tensor(out=V3(bv[0:T, :]), in0=V3(vtok[0:T, :]), in1=bsum[0:T, :].unsqueeze(2).to_broadcast([T, 16, 64]), op=ALU.mult),
                   reads=["bsum"] + [("vtok", h) for h in range(8)], writes=["bv"])
            if first:
                if outs.get("wkv0") is not None:
                    S.dma(stt[:], outs["wkv0"].rearrange("(a h) k v -> (h k) a v", h=2), writes=[STR])
                else:
                    S.pool(lambda e: e.memset(stt[:], 0.0), writes=[STR])
            def hd(g, hl):
                hp = 2 * g + hl // 2; kp = 64 * (hl % 2)
                return hp, kp, slice(kp, kp + 64)
            TP = 64 * nch; psl = slice(0, TP)
            MARK("S1")
            for g in range(4):
                m2 = mT2[psl, :].unsqueeze(1).to_broadcast([TP, 2, 128]); m1 = mL[psl, :].unsqueeze(1).to_broadcast([TP, 2, 64])
                for h2 in range(2):
                    pA, pB, pC = nps(), nps(), nps()
                    for j in range(2):
                        hl = 2 * j + h2
                        hp, kp, ksl = hd(g, hl)
                        for c in range(nch):
                            tp_ = 64 * c; tsl = slice(tp_, tp_ + 64)
                            mm(ps[pA][tsl, j * 128:(j + 1) * 128], bT[ksl, hp, tp_:tp_ + 64], ar[ksl, hp, c, :], True, True, [("bT", hp), ("ar", hp, 0), ("ar", hp, 1)], [("ps", pA)], tp=(kp, tp_))
                            mm(ps[pB][tsl, j * 128:(j + 1) * 128], kTt[ksl, hp, tp_:tp_ + 64], ar[ksl, hp, c, :], True, True, [("kT", hp), ("ar", hp, 0), ("ar", hp, 1)], [("ps", pB)], tp=(kp, tp_))
                            mm(ps[pC][tsl, j * 64:(j + 1) * 64], ar[ksl, hp, c, 0:64], bT[ksl, hp, tp_:tp_ + 64], True, True, [("bT", hp), ("ar", hp, 0)], [("ps", pC)], tp=(kp, tp_))
                    S.dve(lambda e, g=g, pA=pA, m2=m2, h2=h2: e.tensor_tensor(out=AB4[psl, g, h2::2, :], in0=ps[pA][psl, 0:256].rearrange("p (h t) -> p h t", t=128), in1=m2, op=ALU.mult), reads=[("ps", pA), "mT2"], writes=[("AB4", g)])
                    S.dve(lambda e, g=g, pB=pB, m2=m2, h2=h2: e.tensor_tensor(out=AK4[psl, g, h2::2, :], in0=ps[pB][psl, 0:256].rearrange("p (h t) -> p h t", t=128), in1=m2, op=ALU.mult), reads=[("ps", pB), "mT2"], writes=[("AK4", g)])
                    S.dve(lambda e, g=g, pC=pC, m1=m1, h2=h2: e.tensor_tensor(out=Np[psl, g, 0, h2::2, :], in0=ps[pC][psl, 0:128].rearrange("p (h t) -> p h t", t=64), in1=m1, op=ALU.mult), reads=[("ps", pC), "mL"], writes=[("Np", g)])
                S.pool(lambda e, g=g: e.tensor_copy(out=NTp[psl, g, 0], in_=AB4[psl, g, :, 0:64]), reads=[("AB4", g)], writes=[("NTp", g)])
            MARK("S3")
            for l in range(5):
                for g in range(4):
                    pP, pQ = nps(), nps()
                    for hl in range(4):
                        for c in range(nch):
                            tp_ = 64 * c; tsl = slice(tp_, tp_ + 64)
                            mm(ps[pP][tsl, hl * 64:(hl + 1) * 64], NTp[tsl, g, l, hl, :], Np[tsl, g, l % 2, hl, :], True, True, [("NTp", g), ("Np", g)], [("ps", pP)], tp=(tp_, tp_))
                            mm(ps[pQ][tsl, hl * 64:(hl + 1) * 64], Np[tsl, g, l % 2, hl, :], NTp[tsl, g, l, hl, :], True, True, [("NTp", g), ("Np", g)], [("ps", pQ)], tp=(tp_, tp_))
                    S.dve(lambda e, g=g, l=l, pP=pP: e.tensor_copy(out=Np[psl, g, (l + 1) % 2], in_=ps[pP][psl, 0:256].rearrange("p (h t) -> p h t", t=64)), reads=[("ps", pP)], writes=[("Np", g)])
                    S.act(lambda e, g=g, l=l, pQ=pQ: e.copy(out=NTp[psl, g, l + 1], in_=ps[pQ][psl, 0:256].rearrange("p (h t) -> p h t", t=64)), reads=[("ps", pQ)], writes=[("NTp", g)])
            for c in range(nch):
                tp_ = 64 * c
                tsl = slice(tp_, tp_ + 64)
                MARK("S4")
                for g in range(4):
                    pX2 = nps()
                    for h2 in range(2):
                        pX = nps()
                        for j in range(2):
                            hl = 2 * j + h2
                            hp, kp, ksl = hd(g, hl)
                            mm(ps[pX][tsl, j * 64:(j + 1) * 64], ar[ksl, hp, c, 0:64], stt[ksl, hp, :], True, True, [("ar", hp, 0), STR], [("ps", pX)], tp=(kp, tp_))
                        S.act(lambda e, tsl=tsl, g=g, pX=pX, h2=h2: e.copy(out=X4[tsl, g, h2::2, :], in_=ps[pX][tsl, 0:128].rearrange("p (h t) -> p h t", t=64)), reads=[("ps", pX)], writes=[("X4", g)])
                    for hl in range(4):
                        hp, kp, ksl = hd(g, hl)
                        mm(ps[pX2][tsl, hl * 64:(hl + 1) * 64], AK4[tsl, g, hl, 0:64], vtok[tsl, hp * 128 + kp:hp * 128 + kp + 64], True, True, [("AK4", g), ("vtok", hp)], [("ps", pX2)], tp=(tp_, tp_))
                    S.dve(lambda e, tsl=tsl, g=g, pX2=pX2: e.tensor_tensor(out=X4[tsl, g], in0=ps[pX2][tsl, 0:256].rearrange("p (h t) -> p h t", t=64), in1=X4[tsl, g], op=ALU.add), reads=[("ps", pX2), ("X4", g)], writes=[("X4", g)])
                MARK("S5")
                for l in range(5, -1, -1):
                    for g in range(4):
                        pU = nps()
                        for hl in range(4):
                            mm(ps[pU][tsl, hl * 64:(hl + 1) * 64], NTp[tsl, g, l, hl, :], X4[tsl, g, hl, :], True, True, [("NTp", g), ("X4", g)], [("ps", pU)], tp=(tp_, tp_))
                        S.dve(lambda e, tsl=tsl, g=g, pU=pU: e.tensor_tensor(out=X4[tsl, g], in0=ps[pU][tsl, 0:256].rearrange("p (h t) -> p h t", t=64), in1=X4[tsl, g], op=ALU.add), reads=[("ps", pU), ("X4", g)], writes=[("X4", g)])
                MARK("S6")
                for g in range(4):
                    pY2 = nps()
                    y4 = ytok[tsl, g * 256:(g + 1) * 256].rearrange("p (h v) -> p h v", v=64)
                    for h2 in range(2):
                        pY = nps()
                        for j in range(2):
                            hl = 2 * j + h2
                            hp, kp, ksl = hd(g, hl)
                            mm(ps[pY][tsl, j * 64:(j + 1) * 64], ar[ksl, hp, c, 64:128], stt[ksl, hp, :], True, True, [("ar", hp, 1), STR], [("ps", pY)], tp=(kp, tp_))
                        S.act(lambda e, tsl=tsl, pY=pY, h2=h2, y4=y4: e.copy(out=y4[:, h2::2, :], in_=ps[pY][tsl, 0:128].rearrange("p (h t) -> p h t", t=64)), reads=[("ps", pY)], writes=[("ytok", g, c)])
                    for hl in range(4):
                        hp, kp, ksl = hd(g, hl)
                        vsl = slice(hp * 128 + kp, hp * 128 + kp + 64)
                        mm(ps[pY2][tsl, hl * 64:(hl + 1) * 64], AB4[tsl, g, hl, 64:128], X4[tsl, g, hl, :], True, False, [("AB4", g), ("X4", g)], [("ps", pY2)], tp=(tp_, tp_))
                        mm(ps[pY2][tsl, hl * 64:(hl + 1) * 64], AK4[tsl, g, hl, 64:128], vtok[tsl, vsl], False, True, [("AK4", g), ("vtok", hp)], [("ps", pY2)], tp=(tp_, tp_))
                    S.dve(lambda e, tsl=tsl, g=g, pY2=pY2: e.tensor_tensor(out=ytok[tsl, g * 256:(g + 1) * 256], in0=ps[pY2][tsl, 0:256], in1=ytok[tsl, g * 256:(g + 1) * 256], op=ALU.add), reads=[("ps", pY2), ("ytok", g, c)], writes=[("ytok", g, c)])
                for g in range(4):
                    pS = nps()
                    for hl in range(4):
                        hp, kp, ksl = hd(g, hl)
                        vsl = slice(hp * 128 + kp, hp * 128 + kp + 64)
                        mm(ps[pS][ksl, (hl // 2) * 64:(hl // 2) * 64 + 64], btok[tsl, vsl], X4[tsl, g, hl, :], True, False, [("btok", hp), ("X4", g)], [("ps", pS)], tp=(tp_, kp))
                        mm(ps[pS][ksl, (hl // 2) * 64:(hl // 2) * 64 + 64], ktok[tsl, vsl], vtok[tsl, vsl], False, True, [("ktok", hp), ("vtok", hp)], [("ps", pS)], tp=(tp_, kp))
                    for j in range(2):
                        hp = 2 * g + j
                        wl_ = winc[:, hp, tp_ + 63:tp_ + 64]
                        S.dve(lambda e, tsl=tsl, hp=hp, wl_=wl_: e.tensor_scalar(out=t1s[:], in0=stt[:, hp, :], scalar1=wl_, scalar2=None, op0=ALU.mult), reads=[STR, ("winc", hp)], writes=["t1s"])
                        S.dve(lambda e, tsl=tsl, hp=hp, wl_=wl_, j=j, pS=pS: e.scalar_tensor_tensor(out=stt[:, hp, :], in0=ps[pS][:, j * 64:(j + 1) * 64], scalar=wl_, in1=t1s[:], op0=ALU.mult, op1=ALU.add),
                              reads=[("ps", pS), "t1s", ("winc", hp), STR], writes=[STR])
            YT = [("ytok", g, c) for g in range(4) for c in range(nch)]
            MARK("Bepi")
            load_gb(b_lnx_g, b_lnx_b, 1024)
            y3 = V3(ytok[0:T, :])
            bc = lambda col: st8[0:T, col].unsqueeze(2).to_broadcast([T, 16, 64])
            gst = big[1]
            S.dve(lambda e: e.tensor_reduce(out=gst[0:T, 0:16], in_=y3, axis=AX.X, op=ALU.add), reads=YT, writes=["big1"])
            S.dve(lambda e: e.tensor_scalar(out=gst[0:T, 0:16], in0=gst[0:T, 0:16], scalar1=1.0 / 64, scalar2=None, op0=ALU.mult), reads=["big1"], writes=["big1"])
            S.dve(lambda e: e.tensor_tensor(out=y3, in0=y3, in1=gst[0:T, 0:16].unsqueeze(2).to_broadcast([T, 16, 64]), op=ALU.subtract), reads=YT + ["big1"], writes=["ytokc"])
            S.dve(lambda e: e.tensor_tensor(out=V3(big[0][0:T, 0:1024]), in0=y3, in1=y3, op=ALU.mult), reads=["ytokc", "big0a", "big0b"], writes=["big0a", "big0b"])
            S.dve(lambda e: e.tensor_reduce(out=gst[0:T, 16:32], in_=V3(big[0][0:T, 0:1024]), axis=AX.X, op=ALU.add), reads=["big0a", "big0b", "big1"], writes=["big1"])
            S.dve(lambda e: e.tensor_scalar(out=gst[0:T, 16:32], in0=gst[0:T, 16:32], scalar1=1.0 / 64, scalar2=64e-5, op0=ALU.mult, op1=ALU.add), reads=["big1"], writes=["big1"])
            S.act(lambda e: e.activation(out=gst[0:T, 16:32], in_=gst[0:T, 16:32], func=AF.Sqrt), reads=["big1"], writes=["big1"])
            S.dve(lambda e: e.reciprocal(out=gst[0:T, 16:32], in_=gst[0:T, 16:32]), reads=["big1"], writes=["big1"])
            S.dve(lambda e: e.tensor_tensor(out=y3, in0=y3, in1=gst[0:T, 16:32].unsqueeze(2).to_broadcast([T, 16, 64]), op=ALU.mult), reads=["ytokc", "big1"], writes=["ytokc"])
            S.dve(lambda e: e.tensor_tensor(out=ytok[0:T, :], in0=ytok[0:T, :], in1=lxg[0:T, 0:1024], op=ALU.mult), reads=["ytokc", "gb0"], writes=["ytokc"])
            S.dve(lambda e: e.tensor_tensor(out=ytok[0:T, :], in0=ytok[0:T, :], in1=lxb[0:T, 0:1024], op=ALU.add), reads=["ytokc", "gb1"], writes=["ytokc"])
            S.dve(lambda e: e.tensor_tensor(out=ytok[0:T, :], in0=ytok[0:T, :], in1=bv[0:T, :], op=ALU.add), reads=["ytokc", "bv"], writes=["ytokc"])
            for blk in range(4):
                sgw = wload(Wb_in, "Wb_in", 6144 + blk * 256, 256)
                pg = nps()
                for a in range(16):
                    mm(ps[pg][0:T, 0:256], xt[:, a, 0:T], wb[sgw][:, a, :], a == 0, a == 15, [XR, ("wb", sgw)], [("ps", pg)])
                S.act(lambda e, pg=pg, blk=blk: e.activation(out=big[0][0:T, blk * 256:(blk + 1) * 256], in_=ps[pg][0:T, 0:256], func=AF.Silu), reads=[("ps", pg), "big0a", "big0b"], writes=["big0a" if blk < 2 else "big0b"])
            S.dve(lambda e: e.tensor_tensor(out=x1b[0:T, 0:1024], in0=ytok[0:T, :], in1=big[0][0:T, 0:1024], op=ALU.mult), reads=["ytokc", "big0a", "big0b"], writes=["x1b"])
            for cc in range(8):
                S.pe(lambda e, cc=cc: e.transpose(psTb[:, cc * 128:cc * 128 + T], x1b[0:T, cc * 128:(cc + 1) * 128], idb[0:T, 0:T]), reads=["x1b", "idb"], writes=["psTb"])
            S.act(lambda e: e.copy(out=oT[:, 8:16, 0:T], in_=psTb[:].rearrange("p (a t) -> p a t", t=128)[:, :, 0:T]), reads=["psTb"], writes=[("oT", 8 + i) for i in range(8)])
            OT = [("oT", i) for i in range(16)]
            MARK("outproj")
            S.dma(big[1][0:T, :], x_src, reads=[], writes=["big1"])
            load_gb(ln_g, ln_b, D)
            for blk in range(8):
                sw = wload(Wb_out, "Wb_out", blk * 256, 256)
                po = nps()
                for a in range(16):
                    mm(ps[po][0:T, 0:256], oT[:, a, 0:T], wb[sw][:, a, :], a == 0, a == 15, OT + [("wb", sw)], [("ps", po)])
                S.dve(lambda e, po=po, blk=blk: e.scalar_tensor_tensor(out=big[1][0:T, blk * 256:(blk + 1) * 256], in0=big[1][0:T, blk * 256:(blk + 1) * 256], scalar=ALPHA, in1=ps[po][0:T, 0:256], op0=ALU.mult, op1=ALU.add),
                      reads=[("ps", po), "big1"], writes=["big1"])
            ln_rows(big[1][0:T, :], D, T, big[0][0:T, :], lng, lnb, 1e-5, ["big1"], ["big0a", "big0b"], "lng", "lnb")
            if outs.get("dbg") is not None:
                S.dma(outs["dbg"], big[0][0:T, :], reads=["big0a", "big0b"])
            S.act(lambda e: e.copy(out=x1b[0:T, :], in_=big[0][0:T, :]), reads=["big0a", "big0b"], writes=["x1b"])
            for half in range(2):
                for cc in range(8):
                    S.pe(lambda e, cc=cc, half=half: e.transpose(psTb[:, cc * 128:cc * 128 + T], x1b[0:T, (half * 8 + cc) * 128:(half * 8 + cc + 1) * 128], idb[0:T, 0:T]), reads=["x1b", "idb"], writes=["psTb"])
                S.dve(lambda e, half=half: e.tensor_copy(out=x1T[:, half * 8:half * 8 + 8, 0:T], in_=psTb[:].rearrange("p (a t) -> p a t", t=128)[:, :, 0:T]), reads=["psTb"], writes=[("x1T", half)])
            S.dma(x1T_all[:, row0:row0 + T].rearrange("(a p) t -> p a t", p=128), x1T[:, :, 0:T], reads=[("x1T", 0), ("x1T", 1)], writes=[("x1T_all", row0 // 128)])
            S.dma(x1_all[row0:row0 + T, :], big[0][0:T, :], reads=["big0a", "big0b"], writes=[("x1_all", row0 // 128)])
            if last:
                S.dma(outs["sh"].rearrange("(a p) -> p a", p=128), shlast[:], reads=[SHR], allow_slow_non_contiguous=True)
                S.dma(outs["wkvT"].rearrange("(a h) k v -> (h k) a v", h=2), stt[:], reads=[STR])

        for i in range(NT):
            l0_tile(128, xT_p[:, i * 128:(i + 1) * 128], x_p[i * 128:(i + 1) * 128, :], 0, 0, i == 0, i == NT - 1,
                    {"sh": sh_p, "wkvT": wkvT_p, "dbg": (y_p[i * 128:(i + 1) * 128, :] if not do_l1 else None)}, i * 128)
        l0_tile(64, xT_s, x_s, 0, 1, True, True, {"sh": sh_s, "wkvT": wkvT_s, "va": va_s, "shift0": st_shift, "wkv0": st_wkvT,
                                                   "dbg": (y_s if not do_l1 else None)}, SEQ)
        print("SBUF remaining", nc.sbuf_bytes_remaining)
        if not do_l1:
            info = S.emit()
    if do_l1:
        S.barrier()
        with ExitStack() as st:
            sb = lambda n, s, d=F32: st.enter_context(nc.sbuf_tensor("L1" + n, list(s), d))
            psb = lambda n, s, d=F32: st.enter_context(nc.psum_tensor("L1" + n, list(s), d))
            idf_2 = sb("idf", [128, 128]); idb_2 = sb("idb", [128, 128], BF16)
            S.pool(lambda e: e.memset(idf_2[:], 1.0), writes=["idf"])
            S.pool(lambda e: e.affine_select(out=idf_2[:], in_=idf_2[:], pattern=[[-1, 128]], compare_op=ALU.is_equal, fill=0.0, base=0, channel_multiplier=1), reads=["idf"], writes=["idf"])
            S.dve(lambda e: e.tensor_copy(out=idb_2[:], in_=idf_2[:]), reads=["idf"], writes=["idb"])
            gb0_2 = sb("gb0", [128, D]); gb1_2 = sb("gb1", [128, D])
            lng_2 = gb0_2; lnb_2 = gb1_2
            xT2 = sb("xT2", [128, 16, 128], BF16); xT2b = sb("xT2b", [128, 16, 128], BF16)
            wb_2 = [sb("wb%d" % i, [128, 2, 16, 256], BF16) for i in range(3)]
            ps_2 = [psb("ps%d" % i, [128, 512]) for i in range(5)]
            psO_l = [psb("psO%d" % i, [128, 512]) for i in range(2)]; psTb_2 = psb("psTb", [128, 1024], BF16)
            big_2 = [sb("big%d" % i, [128, 2048]) for i in range(2)]
            x1b_2 = sb("x1b", [128, D], BF16); oT_2 = sb("oT", [128, 16, 128], BF16); st8_2 = sb("st8", [128, 16])
            qT = sb("qT", [128, 16, 128], BF16)
            NKMAX = PAST + 64
            KTt = [sb("KTt%d" % i, [128, NKMAX], BF16) for i in range(4)]
            Pm_ = [sb("P%d" % i, [128, NKMAX], BF16) for i in range(4)]
            Vt2 = [sb("Vt%d" % i, [128, NKMAX // 128 + 1, 256], BF16) for i in range(2)]
            attnT = [sb("attnT%d" % i, [128, 8, 128], BF16) for i in range(2)]
            maskb = sb("maskb", [128, 256], BF16); selt = sb("selt", [128, 2]); lamt = sb("lamt", [128, 512]); lamc = sb("lamc", [128, 8]); sgw = sb("sgw", [128, 256])
            mx = sb("mx", [128, 2, 2, 16]); rs = sb("rs", [128, 2, 2, 16]); smt = sb("sm", [128, 2, 8])
            S.dma(maskb[:], mask2, writes=["maskb"], q="pool")
            S.dma(selt[:], sel_in, writes=["selt"])
            S.dma(lamt[:], lamv.partition_broadcast(128), writes=["lamt"])
            S.dma(sgw[:], subln_g.partition_broadcast(128), writes=["sgw"])
            S.dve(lambda e: e.tensor_scalar(out=sgw[:], in0=sgw[:], scalar1=1.0 - LAM_INIT, scalar2=None, op0=ALU.mult), reads=["sgw"], writes=["sgw"])
            for i in range(2):
                S.dve(lambda e, i=i: e.tensor_tensor(out=lamt[:, i * 256:i * 256 + 128], in0=lamt[:, i * 256:i * 256 + 128], in1=lamt[:, i * 256 + 128:i * 256 + 256], op=ALU.mult), reads=["lamt"], writes=["lamt"])
                S.dve(lambda e, i=i: e.tensor_reduce(out=lamc[:, 2 + i:3 + i], in_=lamt[:, i * 256:i * 256 + 128], axis=AX.X, op=ALU.add), reads=["lamt"], writes=["lamc"])
            S.act(lambda e: e.activation(out=lamc[:, 4:6], in_=lamc[:, 2:4], func=AF.Exp), reads=["lamc"], writes=["lamc"])
            S.dve(lambda e: e.tensor_tensor(out=lamc[:, 0:1], in0=lamc[:, 4:5], in1=lamc[:, 5:6], op=ALU.subtract), reads=["lamc"], writes=["lamc"])
            S.dve(lambda e: e.tensor_scalar(out=lamc[:, 1:2], in0=lamc[:, 0:1], scalar1=LAM_INIT, scalar2=-1.0, op0=ALU.add, op1=ALU.mult), reads=["lamc"], writes=["lamc"])

            wb_rr_2 = [0]; ps_rr_2 = [0]
            def load_gb_2(gsrc, bsrc, w):
                S.dma(gb0_2[:, 0:w], gsrc.partition_broadcast(128), writes=["gb0"])
                S.dma(gb1_2[:, 0:w], bsrc.partition_broadcast(128), writes=["gb1"])
            def wload_2(W, nm, c0, n):
                s_ = wb_rr_2[0] % len(wb_2); wb_rr_2[0] += 1
                assert c0 % 512 == 0 and n == 512
                S.dma(wb_2[s_][:, :, :, :], W[c0 // 256:c0 // 256 + 2].rearrange("b p a c -> p b a c"), reads=WRES(nm), writes=[("wb", s_)])
                return s_

            def nps_2():
                i = ps_rr_2[0] % len(ps_2); ps_rr_2[0] += 1
                return i

            def mm_2(out, lhsT, rhs, start, stop, reads, writes, tp=None):
                if tp is None:
                    S.pe(lambda e: e.matmul(out, lhsT=lhsT, rhs=rhs, start=start, stop=stop), reads, writes)
                else:
                    S.pe(lambda e: e.matmul(out, lhsT=lhsT, rhs=rhs, start=start, stop=stop, tile_position=tp), reads, writes)

            def ln_rows_2(src_ap, width, T, dst_ap, gt, bt_, eps, rsrc, rdst, gn, bn):
                junk = x1b_2
                S.dve(lambda e: e.memset(st8_2[0:T, 0:2], 0.0), writes=["st8a", "st8b"])
                S.act(lambda e: e.activation(out=junk[0:T, 0:width], in_=src_ap, func=AF.Identity, accum_out=st8_2[0:T, 0:1]), reads=rsrc + ["st8a"], writes=["x1b", "st8a"])
                S.act(lambda e: e.activation(out=junk[0:T, 0:width], in_=src_ap, func=AF.Square, accum_out=st8_2[0:T, 1:2]), reads=rsrc + ["x1b", "st8b"], writes=["x1b", "st8b"])
                S.dve(lambda e: e.tensor_scalar(out=st8_2[0:T, 2:3], in0=st8_2[0:T, 0:1], scalar1=1.0 / width, scalar2=None, op0=ALU.mult), reads=["st8a"], writes=["st8c"])
                S.dve(lambda e: e.tensor_tensor(out=st8_2[0:T, 3:4], in0=st8_2[0:T, 2:3], in1=st8_2[0:T, 2:3], op=ALU.mult), reads=["st8c"], writes=["st8d"])
                S.dve(lambda e: e.scalar_tensor_tensor(out=st8_2[0:T, 4:5], in0=st8_2[0:T, 1:2], scalar=1.0 / width, in1=st8_2[0:T, 3:4], op0=ALU.mult, op1=ALU.subtract), reads=["st8b", "st8d"], writes=["st8e"])
                S.dve(lambda e: e.tensor_scalar(out=st8_2[0:T, 4:5], in0=st8_2[0:T, 4:5], scalar1=eps, scalar2=None, op0=ALU.add), reads=["st8e"], writes=["st8e"])
                S.act(lambda e: e.activation(out=st8_2[0:T, 5:6], in_=st8_2[0:T, 4:5], func=AF.Sqrt), reads=["st8e"], writes=["st8f"])
                S.dve(lambda e: e.reciprocal(out=st8_2[0:T, 6:7], in_=st8_2[0:T, 5:6]), reads=["st8f"], writes=["st8g"])
                S.dve(lambda e: e.scalar_tensor_tensor(out=st8_2[0:T, 7:8], in0=st8_2[0:T, 2:3], scalar=-1.0, in1=st8_2[0:T, 6:7], op0=ALU.mult, op1=ALU.mult), reads=["st8c", "st8g"], writes=["st8h"])
                S.act(lambda e: e.activation(out=dst_ap, in_=src_ap, func=AF.Identity, bias=st8_2[0:T, 7:8], scale=st8_2[0:T, 6:7]), reads=rsrc + ["st8g", "st8h"], writes=rdst)
                S.dve(lambda e: e.tensor_tensor(out=dst_ap, in0=dst_ap, in1=gt[0:T, 0:width], op=ALU.mult), reads=rdst + ["gb0"], writes=rdst)
                S.dve(lambda e: e.tensor_tensor(out=dst_ap, in0=dst_ap, in1=bt_[0:T, 0:width], op=ALU.add), reads=rdst + ["gb1"], writes=rdst)


            def load_x1T(T, row0):
                S.dma(xT2[:, :, 0:T], x1T_all[:, row0:row0 + T].rearrange("(a p) t -> p a t", p=128), reads=[("x1T_all", row0 // 128)], writes=["xT2"])

            def proj_tok(T, c0, blk, evac):
                s_ = wload_2(Wc_in, "Wc_in", c0 + blk * 512, 512); pi = nps_2()
                for a in range(16):
                    mm_2(ps_2[pi][0:T, :], xT2[:, a, 0:T], wb_2[s_][:, :, a, :], a == 0, a == 15, ["xT2", ("wb", s_)], [("ps", pi)])
                evac(pi)

            def to_qT(T):
                for half in range(2):
                    for cc in range(8):
                        S.pe(lambda e, cc=cc, half=half: e.transpose(psTb_2[:, cc * 128:cc * 128 + T], x1b_2[0:T, (half * 8 + cc) * 128:(half * 8 + cc + 1) * 128], idb_2[0:T, 0:T]), reads=["x1b", "idb"], writes=["psTb"])
                    S.dve(lambda e, half=half: e.tensor_copy(out=qT[:, half * 8:half * 8 + 8, 0:T], in_=psTb_2[:].rearrange("p (a t) -> p a t", t=128)[:, :, 0:T]), reads=["psTb"], writes=[("qT", half)])
            QT = [("qT", 0), ("qT", 1)]

            def kv_tile(T, row0, k_out, v_out):
                load_x1T(T, row0)
                for which, c0, dst_out in ((0, 2048, k_out), (1, 4096, v_out)):
                    dst = big_2[which]; rn = ["big0a", "big0b"] if which == 0 else ["big1"]
                    for blk in range(4):
                        proj_tok(T, c0, blk, lambda pi, blk=blk, dst=dst, rn=rn: S.act(lambda e: e.copy(out=dst[0:T, blk * 512:(blk + 1) * 512], in_=ps_2[pi][0:T, :]), reads=[("ps", pi)], writes=rn))
                    S.dma(dst_out, dst[0:T, :], reads=rn)
                    S.dve(lambda e, dst=dst: e.tensor_copy(out=x1b_2[0:T, :], in_=dst[0:T, :]), reads=rn, writes=["x1b"])
                    if which == 0:
                        to_qT(T)
                        S.dma(KT[:, :, row0:row0 + T].rearrange("j p t -> p j t"), qT[:, :, 0:T], reads=QT, writes=[("KT", row0 // 128)])
                    else:
                        S.dma(Vb[row0:row0 + T, :], x1b_2[0:T, :], reads=["x1b"], writes=[("Vb", row0 // 128)])

            def kv_pair(rowA, rowB, outs2):
                T = 128
                S.dma(xT2[:, :, 0:T], x1T_all[:, rowA:rowA + T].rearrange("(a p) t -> p a t", p=128), reads=[("x1T_all", rowA // 128)], writes=["xT2"])
                S.dma(xT2b[:, :, 0:T], x1T_all[:, rowB:rowB + T].rearrange("(a p) t -> p a t", p=128), reads=[("x1T_all", rowB // 128)], writes=["xT2b"])
                srcs = ((xT2, "xT2", big_2[0], ["big0a", "big0b"]), (xT2b, "xT2b", big_2[1], ["big1"]))
                rows = (rowA, rowB)
                for which, c0 in ((0, 2048), (1, 4096)):
                    for blk in range(4):
                        s_ = wload_2(Wc_in, "Wc_in", c0 + blk * 512, 512)
                        for (xt_, xr, dst, rn) in srcs:
                            pi = nps_2()
                            for a in range(16):
                                mm_2(ps_2[pi][0:T, :], xt_[:, a, 0:T], wb_2[s_][:, :, a, :], a == 0, a == 15, [xr, ("wb", s_)], [("ps", pi)])
                            S.act(lambda e, pi=pi, dst=dst, blk=blk: e.copy(out=dst[0:T, blk * 512:(blk + 1) * 512], in_=ps_2[pi][0:T, :]), reads=[("ps", pi)], writes=rn)
                    for ti, (xt_, xr, dst, rn) in enumerate(srcs):
                        row0 = rows[ti]
                        S.dma(outs2[ti][which], dst[0:T, :], reads=rn)
                        S.dve(lambda e, dst=dst: e.tensor_copy(out=x1b_2[0:T, :], in_=dst[0:T, :]), reads=rn, writes=["x1b"])
                        if which == 0:
                            to_qT(T)
                            S.dma(KT[:, :, row0:row0 + T].rearrange("j p t -> p j t"), qT[:, :, 0:T], reads=QT, writes=[("KT", row0 // 128)])
                        else:
                            S.dma(Vb[row0:row0 + T, :], x1b_2[0:T, :], reads=["x1b"], writes=[("Vb", row0 // 128)])

            def att_tile(T, row0, nk_ctx, is_sample, y_out, rowB=None):
                load_x1T(T, row0)
                if rowB is not None:
                    S.dma(qT[:, :, 0:T], x1T_all[:, rowB:rowB + T].rearrange("(a p) t -> p a t", p=128), reads=[("x1T_all", rowB // 128)], writes=QT)
                    fl = lambda t_: t_[:].rearrange("p a t -> p (a t)")
                    S.dve(lambda e: e.tensor_scalar(out=fl(qT), in0=fl(qT), scalar1=selt[:, 1:2], scalar2=None, op0=ALU.mult), reads=QT + ["selt"], writes=QT)
                    S.dve(lambda e: e.scalar_tensor_tensor(out=fl(xT2), in0=fl(xT2), scalar=selt[:, 0:1], in1=fl(qT), op0=ALU.mult, op1=ALU.add), reads=QT + ["selt", "xT2"], writes=["xT2"])
                for blk in range(4):
                    proj_tok(T, 0, blk, lambda pi, blk=blk: S.act(lambda e: e.activation(out=x1b_2[0:T, blk * 512:(blk + 1) * 512], in_=ps_2[pi][0:T, :], func=AF.Copy, scale=float(128 ** -0.5)), reads=[("ps", pi)], writes=["x1b"]))
                to_qT(T)
                for blk in range(4):
                    proj_tok(T, 6144, blk, lambda pi, blk=blk: S.act(lambda e: e.activation(out=big_2[0][0:T, blk * 512:(blk + 1) * 512], in_=ps_2[pi][0:T, :], func=AF.Silu), reads=[("ps", pi)], writes=["big0a", "big0b"]))
                S.dve(lambda e: e.tensor_tensor(out=big_2[0][0:T, :].rearrange("p (h e) -> p h e", e=256), in0=big_2[0][0:T, :].rearrange("p (h e) -> p h e", e=256),
                                                in1=sgw[0:T, :].unsqueeze(1).to_broadcast([T, 8, 256]), op=ALU.mult), reads=["big0a", "big0b", "sgw"], writes=["big0a", "big0b"])
                if is_sample:
                    nk = PAST + 64; nkt = PAST // 128 + 1
                else:
                    nk = nk_ctx; nkt = nk // 128
                nb = (nk + 511) // 512
                def stageA(h):
                    par = h % 2; hs_ = slice(h * 256, (h + 1) * 256); Vt = Vt2[par]; VR = ("Vt", par); sm = smt[:, par, :]
                    if is_sample:
                        S.dma(Vt[:, 0:PAST // 128, :], vc[:, hs_].rearrange("(kt p) e -> p kt e", p=128), writes=[VR], q="pool")
                        S.dma(Vt[0:64, PAST // 128, :], Vb[SEQ:SEQ + 64, hs_], reads=[("Vb", SEQ // 128)], writes=[VR])
                    else:
                        S.dma(Vt[:, 0:nkt, :], Vb[0:nk, hs_].rearrange("(kt p) e -> p kt e", p=128), reads=[("Vb", i) for i in range(nkt)], writes=[VR])
                    for jm in range(2):
                        hj = 2 * h + jm; kt_ = KTt[2 * par + jm]; KR = ("KTt", par, jm)
                        if is_sample:
                            S.dma(kt_[:, 0:PAST], kcT[hj], writes=[KR], q="pool")
                            S.dma(kt_[:, PAST:PAST + 64], KT[hj, :, SEQ:SEQ + 64], reads=[("KT", SEQ // 128)], writes=[KR])
                        else:
                            S.dma(kt_[:, 0:nk], KT[hj, :, 0:nk], reads=[("KT", i) for i in range(nkt)], writes=[KR])

                    def smm(pi, b, jm):
                        hj = 2 * h + jm; kt_ = KTt[2 * par + jm]; KR = ("KTt", par, jm)
                        w = min(512, nk - b * 512)
                        diag = (not is_sample) and (b == nb - 1)
                        mm_2(ps_2[pi][0:T, 0:w], qT[:, hj, 0:T], kt_[:, b * 512:b * 512 + w], True, not diag, [("qT", hj // 8), KR], [("ps", pi)])
                        if diag:
                            mm_2(ps_2[pi][0:T, w - 256:w], idb_2[0:T, 0:T], maskb[0:T, :], False, True, ["idb", "maskb"], [("ps", pi)])
                        return w
                    for b in range(nb):
                        for jm in range(2):
                            pi = nps_2(); w = smm(pi, b, jm)
                            S.dve(lambda e, pi=pi, w=w, b=b, jm=jm: e.tensor_reduce(out=mx[0:T, par, jm, b:b + 1], in_=ps_2[pi][0:T, 0:w], axis=AX.X, op=ALU.max), reads=[("ps", pi)], writes=[("mx", par, jm)])
                    for jm in range(2):
                        S.dve(lambda e, jm=jm: e.tensor_reduce(out=sm[0:T, jm:jm + 1], in_=mx[0:T, par, jm, 0:nb], axis=AX.X, op=ALU.max), reads=[("mx", par, jm)], writes=[("sm", par, jm)])
                        S.dve(lambda e, jm=jm: e.tensor_scalar(out=sm[0:T, jm:jm + 1], in0=sm[0:T, jm:jm + 1], scalar1=-1.0, scalar2=None, op0=ALU.mult), reads=[("sm", par, jm)], writes=[("sm", par, jm)])
                    for b in range(nb):
                        for jm in range(2):
                            pi = nps_2(); w = smm(pi, b, jm); Pm = Pm_[2 * par + jm]
                            S.act(lambda e, pi=pi, w=w, b=b, jm=jm, Pm=Pm: e.activation(out=Pm[0:T, b * 512:b * 512 + w], in_=ps_2[pi][0:T, 0:w], func=AF.Exp, bias=sm[0:T, jm:jm + 1], scale=1.0, accum_out=rs[0:T, par, jm, b:b + 1]),
                                  reads=[("ps", pi), ("sm", par, jm)], writes=[("P", par, jm), ("rs", par, jm)])
                    for jm in range(2):
                        S.dve(lambda e, jm=jm: e.tensor_reduce(out=sm[0:T, 2 + jm:3 + jm], in_=rs[0:T, par, jm, 0:nb], axis=AX.X, op=ALU.add), reads=[("rs", par, jm)], writes=[("sm", par, 2 + jm)])
                    P0 = Pm_[2 * par]; P1 = Pm_[2 * par + 1]
                    S.dve(lambda e: e.reciprocal(out=sm[0:T, 4:6], in_=sm[0:T, 2:4]), reads=[("sm", par, 2), ("sm", par, 3)], writes=[("sm", par, 4)])
                    S.dve(lambda e: e.tensor_tensor(out=sm[0:T, 5:6], in0=sm[0:T, 5:6], in1=lamc[0:T, 1:2], op=ALU.mult), reads=[("sm", par, 4), "lamc"], writes=[("sm", par, 4)])
                    S.dve(lambda e: e.tensor_scalar(out=P1[0:T, 0:nk], in0=P1[0:T, 0:nk], scalar1=sm[0:T, 5:6], scalar2=None, op0=ALU.mult), reads=[("P", par, 1), ("sm", par, 4)], writes=[("P", par, 1)])
                    S.dve(lambda e: e.scalar_tensor_tensor(out=P0[0:T, 0:nk], in0=P0[0:T, 0:nk], scalar=sm[0:T, 4:5], in1=P1[0:T, 0:nk], op0=ALU.mult, op1=ALU.add), reads=[("P", par, 0), ("P", par, 1), ("sm", par, 4)], writes=[("P", par, 0)])

                def stageB(h):
                    par = h % 2; hs_ = slice(h * 256, (h + 1) * 256); Vt = Vt2[par]; VR = ("Vt", par); sm = smt[:, par, :]
                    P0 = Pm_[2 * par]; psO = psO_l[par]; OR_ = ("psO", par); EP = ("ep", par); c0 = par * 512
                    for kt0 in range(0, nkt, 8):
                        n8 = min(8, nkt - kt0); ai = (kt0 // 8) % 2; aT = attnT[ai]
                        for i in range(n8):
                            kt = kt0 + i; kw = min(128, nk - kt * 128)
                            S.pe(lambda e, i=i, kt=kt, kw=kw: e.transpose(psTb_2[0:kw, i * 128:i * 128 + T], P0[0:T, kt * 128:kt * 128 + kw], idb_2[0:T, 0:T]), reads=[("P", par, 0), "idb"], writes=["psTb"])
                        S.act(lambda e, n8=n8, aT=aT: e.copy(out=aT[:, 0:n8, 0:T], in_=psTb_2[:].rearrange("p (a t) -> p a t", t=128)[:, 0:n8, 0:T]), reads=["psTb"], writes=[("attnT", ai)])
                        for i in range(n8):
                            kt = kt0 + i; kw = min(128, nk - kt * 128)
                            mm_2(psO[0:T, 0:256], aT[0:kw, i, 0:T], Vt[0:kw, kt, :], kt == 0, kt == nkt - 1, [("attnT", ai), VR], [OR_])
                    S.act(lambda e: e.activation(out=big_2[1][0:T, c0:c0 + 256], in_=psO[0:T, 0:256], func=AF.Square, accum_out=sm[0:T, 6:7]), reads=[OR_], writes=[EP, ("sm", par, 6)])
                    S.dve(lambda e: e.tensor_scalar(out=sm[0:T, 6:7], in0=sm[0:T, 6:7], scalar1=1.0 / 256, scalar2=1e-5, op0=ALU.mult, op1=ALU.add), reads=[("sm", par, 6)], writes=[("sm", par, 6)])
                    S.act(lambda e: e.activation(out=sm[0:T, 6:7], in_=sm[0:T, 6:7], func=AF.Sqrt), reads=[("sm", par, 6)], writes=[("sm", par, 6)])
                    S.dve(lambda e: e.reciprocal(out=sm[0:T, 7:8], in_=sm[0:T, 6:7]), reads=[("sm", par, 6)], writes=[("sm", par, 7)])
                    S.dve(lambda e: e.scalar_tensor_tensor(out=x1b_2[0:T, hs_], in0=psO[0:T, 0:256], scalar=sm[0:T, 7:8], in1=big_2[0][0:T, hs_], op0=ALU.mult, op1=ALU.mult), reads=[OR_, ("sm", par, 7), "big0a", "big0b"], writes=["x1b"])

                stageA(0)
                for h in range(8):
                    if h + 1 < 8:
                        stageA(h + 1)
                    stageB(h)
                for half in range(2):
                    for cc in range(8):
                        S.pe(lambda e, cc=cc, half=half: e.transpose(psTb_2[:, cc * 128:cc * 128 + T], x1b_2[0:T, (half * 8 + cc) * 128:(half * 8 + cc + 1) * 128], idb_2[0:T, 0:T]), reads=["x1b", "idb"], writes=["psTb"])
                    S.dve(lambda e, half=half: e.tensor_copy(out=oT_2[:, half * 8:half * 8 + 8, 0:T], in_=psTb_2[:].rearrange("p (a t) -> p a t", t=128)[:, :, 0:T]), reads=["psTb"], writes=[("oT", half)])
                OT2 = [("oT", 0), ("oT", 1)]
                S.dma(big_2[1][0:T, :], x1_all[row0:row0 + T, :], reads=[("x1_all", row0 // 128)], writes=["big1", ("ep", 0), ("ep", 1)])
                if rowB is not None:
                    S.dma(big_2[0][0:T, :], x1_all[rowB:rowB + T, :], reads=[("x1_all", rowB // 128)], writes=["big0a", "big0b"])
                    S.dve(lambda e: e.tensor_scalar(out=big_2[0][0:T, :], in0=big_2[0][0:T, :], scalar1=selt[0:T, 1:2], scalar2=None, op0=ALU.mult), reads=["big0a", "big0b", "selt"], writes=["big0a", "big0b"])
                    S.dve(lambda e: e.scalar_tensor_tensor(out=big_2[1][0:T, :], in0=big_2[1][0:T, :], scalar=selt[0:T, 0:1], in1=big_2[0][0:T, :], op0=ALU.mult, op1=ALU.add), reads=["big0a", "big0b", "big1", "selt"], writes=["big1"])
                load_gb_2(cln_g, cln_b, D)
                for blk in range(4):
                    sw = wload_2(Wc_out, "Wc_out", blk * 512, 512); po = nps_2()
                    for a in range(16):
                        mm_2(ps_2[po][0:T, :], oT_2[:, a, 0:T], wb_2[sw][:, :, a, :], a == 0, a == 15, OT2 + [("wb", sw)], [("ps", po)])
                    S.dve(lambda e, po=po, blk=blk: e.scalar_tensor_tensor(out=big_2[1][0:T, blk * 512:(blk + 1) * 512], in0=big_2[1][0:T, blk * 512:(blk + 1) * 512], scalar=ALPHA, in1=ps_2[po][0:T, :], op0=ALU.mult, op1=ALU.add),
                          reads=[("ps", po), "big1"], writes=["big1"])
                ln_rows_2(big_2[1][0:T, :], D, T, big_2[0][0:T, :], lng_2, lnb_2, 1e-5, ["big1"], ["big0a", "big0b"], "lng", "lnb")
                S.dma(y_out, big_2[0][0:T, :], reads=["big0a", "big0b"])

            for i in range(0, NT, 2):
                kv_pair(i * 128, (i + 1) * 128, [(k_p[r * 128:(r + 1) * 128, :], v_p[r * 128:(r + 1) * 128, :]) for r in (i, i + 1)])
            kv_tile(64, SEQ, k_s, v_s)
            for j in range(NT // 2):
                att_tile(128, 2 * j * 128, (2 * j + 2) * 128, False, y_p[j * 128:(j + 1) * 128, :], rowB=(2 * j + 1) * 128)
            att_tile(64, SEQ, 0, True, y_s)
            print("SBUF remaining L1", nc.sbuf_bytes_remaining)
            info = S.emit()
    import os
    if os.environ.get("K_MARKS"): print(MARKS[:40])
    return nc, info


_CACHE = {}
NCORES = 8
DO_L1 = True


def kernel(**inp):
    f = lambda a: np.ascontiguousarray(np.asarray(a, dtype=np.float32))
    do_l1 = DO_L1
    if "nc" not in _CACHE:
        _CACHE["nc"] = build(do_l1)
    nc, info = _CACHE["nc"]
    xp = f(inp["x_prompt"]); xs = f(inp["x_sample"])
    common = {
        "w_in": f(np.concatenate([np.asarray(inp["ab_w_in"][0])[:, 0:6144], np.asarray(inp["ab_w_in"][0])[:, 6272:7296], np.asarray(inp["ab_w_in"][0])[:, 6144:6272]], axis=1)), "w_out": f(inp["ab_w_out"][0]), "cw_in": f(inp["c_w_in"][0]), "cw_out": f(inp["c_w_out"][0]),
        "a_ln_g": f(inp["ab_a_ln_g"]), "a_ln_b": f(inp["ab_a_ln_b"]),
        "a_wsT": f(np.transpose(np.asarray(inp["ab_a_ws"][0]), (0, 2, 1))), "a_bs": f(np.asarray(inp["ab_a_bs"][0]).reshape(1, 512)),
        "b_mu": f(inp["ab_b_mu"][0]), "b_w0": f(inp["ab_b_w0"][0]), "b_w2": f(inp["ab_b_w2"][0]), "b_a0": f(inp["ab_b_a0"][0]),
        "b_a2": f(inp["ab_b_a2"][0]), "b_kk": f(inp["ab_b_kk"][0]), "b_ka": f(inp["ab_b_ka"][0]), "b_rk": f(np.asarray(inp["ab_b_rk"][0]).reshape(1024)),
        "b_lnx_g": f(inp["ab_b_lnx_g"]), "b_lnx_b": f(inp["ab_b_lnx_b"]),
        "ln_g": f(inp["ab_ln_g"]), "ln_b": f(inp["ab_ln_b"]), "cln_g": f(inp["c_ln_g"]), "cln_b": f(inp["c_ln_b"]),
        "lamv": f(np.concatenate([np.asarray(inp[k][0]) for k in ("c_lam_q1", "c_lam_k1", "c_lam_q2", "c_lam_k2")]).reshape(1, 512)),
        "subln_g": f(inp["c_subln_g"]),
    }
    ck = np.asarray(inp["cache_c_k"][0]); cv = np.asarray(inp["cache_c_v"][0])
    in_maps = []
    NB = max(1, NCORES // 2)
    for c in range(NCORES):
        b = c // 2
        m = dict(common)
        md = np.where((np.arange(128)[None, :] // 64) <= (np.arange(128)[:, None] // 64), 0.0, -30000.0)
        pp = c % 2
        m["mask2"] = f(np.concatenate([md, np.full((128, 128), -30000.0)], 1) if pp == 0 else np.concatenate([np.zeros((128, 128)), md], 1))
        m["sel"] = f(np.tile(np.array([[1.0 - pp, float(pp)]]), (128, 1)))
        m["xT_p"] = f(xp[b, :SEQ].T); m["x_p"] = f(xp[b, :SEQ]); m["xT_s"] = f(xs[c].T); m["x_s"] = f(xs[c])
        m["st_shift"] = f(inp["state_b_shift"][0, c]); m["st_wkvT"] = f(np.transpose(np.asarray(inp["state_b_wkv"][0, c]), (0, 2, 1)))
        if DO_L1:
            m["kcT"] = f(np.transpose(ck[c].reshape(PAST, 16, 128), (1, 2, 0))); m["vc"] = f(cv[c].reshape(PAST, 2048))
        in_maps.append(m)
    import os
    if os.environ.get("K_TRACE"):
        res = run_bass_kernel_spmd(nc, in_maps, core_ids=list(range(NCORES)), trace=True)
        print("EXEC_TIME_NS", res.exec_time_ns)
        global LAST_RES
        LAST_RES = res
    else:
        res = run_bass_kernel_spmd(nc, in_maps, core_ids=list(range(NCORES)))
    R = res.results
    global LAST_R
    LAST_R = R
    T3 = lambda a: np.transpose(a, (0, 2, 1))
    if DO_L1:
        y_prompt = np.stack([np.stack([R[2 * b + (1 if NCORES > 1 else 0) * q]["y_p"].reshape(SEQ // 256, 128, D) for q in range(2)], axis=1).reshape(SEQ, D) for b in range(NB)])
    else:
        y_prompt = np.stack([R[2 * b]["y_p"] for b in range(NB)])
    y_sample = np.stack([R[c]["y_s"] for c in range(NCORES)])
    sh_p = np.stack([R[2 * b]["sh_p"] for b in range(NB)])[None]
    wkv_p = np.stack([T3(R[2 * b]["wkvT_p"]) for b in range(NB)])[None]
    sh_s = np.stack([R[c]["sh_s"] for c in range(NCORES)])[None]
    wkv_s = np.stack([T3(R[c]["wkvT_s"]) for c in range(NCORES)])[None]
    va_s = np.stack([R[c]["va_s"] for c in range(NCORES)])[None]
    k_p = np.stack([R[2 * b]["k_p"] for b in range(NB)]).reshape(1, NB, SEQ, 8, 2, 128)
    v_p = np.stack([R[2 * b]["v_p"] for b in range(NB)]).reshape(1, NB, SEQ, 8, 256)
    k_s = np.stack([R[c]["k_s"] for c in range(NCORES)]).reshape(1, NCORES, 64, 8, 2, 128)
    v_s = np.stack([R[c]["v_s"] for c in range(NCORES)]).reshape(1, NCORES, 64, 8, 256)
    return (y_prompt, y_sample, sh_p, wkv_p, sh_s, wkv_s, va_s, k_p, v_p, k_s, v_s)
```

```python
from contextlib import ExitStack
import numpy as np
import concourse.bass as bass
import concourse.mybir as mybir
from concourse.bass_utils import run_bass_kernel_spmd

F32 = mybir.dt.float32
BF16 = mybir.dt.bfloat16
AF = mybir.ActivationFunctionType
ALU = mybir.AluOpType
AX = mybir.AxisListType
ENGS = ("pe", "act", "dve", "pool", "sp")
SEM_ROT = 30000

D = 2048
SEQ = 4096
NT = SEQ // 128
PAST = 4096
PAB = 7296
ALPHA = 4.0 ** 0.25
LAM_INIT = 0.8 - 0.6 * float(np.exp(-0.3))
C_DEC = float(np.exp(-0.5))


class Op:
    __slots__ = ("eng", "fn", "deps", "signaled", "token", "dma", "waits")

    def __init__(self, eng, fn, dma):
        self.eng = eng; self.fn = fn; self.deps = []; self.signaled = False
        self.token = None; self.dma = dma; self.waits = []


class Sched:
    def __init__(self, nc):
        self.nc = nc; self.ops = []; self.res = {}
        self.ndma = {"sp": 24, "pool": 12, "act": 4, "pe": 1, "dve": 1}
        self.bar_deps = []; self.need_bar = set()
        self.last = {}; self.dmas = []

    def barrier(self):
        self.bar_deps = list(self.last.values()) + self.dmas
        self.dmas = []; self.need_bar = set(ENGS)

    def op(self, eng, fn, reads=(), writes=(), dma=False):
        o = Op(eng, fn, dma); deps = {}

        def add(d):
            if d is None: return
            if (not d.dma) and d.eng == "pe" and eng == "pe" and not dma: return
            deps[id(d)] = d
        if eng in self.need_bar:
            self.need_bar.discard(eng)
            for d in self.bar_deps: deps[id(d)] = d
        for r in reads:
            st = self.res.get(r)
            if st is not None: add(st[0])
        for w in writes:
            st = self.res.get(w)
            if st is not None:
                add(st[0])
                for rd in st[1].values(): add(rd)
        o.deps = list(deps.values())
        for d in o.deps: d.signaled = True
        for r in reads:
            st = self.res.setdefault(r, [None, {}])
            st[1][("dma", id(o)) if dma else eng] = o
        for w in writes: self.res[w] = [o, {}]
        if dma:
            o.signaled = True; self.dmas.append(o)
        else:
            self.last[eng] = o
        self.ops.append(o)
        return o

    def pe(self, fn, reads=(), writes=()): return self.op("pe", fn, reads, writes)
    def act(self, fn, reads=(), writes=()): return self.op("act", fn, reads, writes)
    def dve(self, fn, reads=(), writes=()): return self.op("dve", fn, reads, writes)
    def pool(self, fn, reads=(), writes=()): return self.op("pool", fn, reads, writes)

    def dma(self, out, in_, reads=(), writes=(), q="sp", **kw):
        return self.op(q, lambda e: e.dma_start(out=out, in_=in_, **kw), reads, writes, dma=True)

    def finalize(self):
        cnt = {e: 0 for e in ENGS}; rr = {e: 0 for e in ENGS}
        dcnt = {}; dlast = {}; per = {e: [] for e in ENGS}; waited = {e: {} for e in ENGS}; keys = set()
        for o in self.ops:
            e = o.eng; w = waited[e]; need = {}
            for d in o.deps:
                k, v = d.token
                if w.get(k, 0) < v and need.get(k, 0) < v: need[k] = v
            if o.dma:
                n = self.ndma[e]; slot = rr[e] % n; rr[e] += 1
                c = dcnt.get((e, slot), 0); gen, idx = divmod(c, 1800)
                k = ("d", e, slot, gen); prev = dlast.get((e, slot))
                if prev is not None:
                    pk, pv = prev
                    if w.get(pk, 0) < pv and need.get(pk, 0) < pv: need[pk] = pv
                o.token = (k, 16 * (idx + 1)); dcnt[(e, slot)] = c + 1; dlast[(e, slot)] = o.token; keys.add(k)
            elif o.signaled:
                c = cnt[e]; gen, idx = divmod(c, SEM_ROT); k = ("c", e, gen)
                o.token = (k, idx + 1); cnt[e] = c + 1; keys.add(k)
            for k, v in need.items():
                w[k] = v; o.waits.append((k, v))
            per[e].append(o)
        self.per = per; self.keys = sorted(keys, key=str); self.finals = list(dlast.values())
        for e in ENGS:
            if cnt[e]:
                gen, idx = divmod(cnt[e] - 1, SEM_ROT); self.finals.append((("c", e, gen), idx + 1))

    def emit(self):
        import os
        mo = int(os.environ.get('K_MAXOPS', '0'))
        if mo: self.ops = self.ops[:mo]
        nc = self.nc; self.finalize()
        with ExitStack() as st:
            sems = {k: st.enter_context(nc.semaphore("s%d" % i)) for i, k in enumerate(self.keys)}
            block = st.enter_context(nc.Block()); per = self.per; finals = self.finals

            def run(en, e, last=False):
                for o in per[en]:
                    for (k, v) in o.waits: e.wait_ge(sems[k], v)
                    ins = o.fn(e)
                    if o.token is not None: ins.then_inc(sems[o.token[0]], 16 if o.dma else 1)
                if last:
                    for (k, v) in finals: e.wait_ge(sems[k], v)

            @block.tensor
            def _(e): run("pe", e)

            @block.scalar
            def _(e): run("act", e)

            @block.vector
            def _(e): run("dve", e)

            @block.gpsimd
            def _(e): run("pool", e)

            @block.sync
            def _(e): run("sp", e, last=True)
        return {e: len(per[e]) for e in ENGS}, len(self.keys)


def build(do_l1=True):
    nc = bass.Bass("TRN2", target_bir_lowering=False)
    S = Sched(nc)
    din = lambda n, s, dt=F32: nc.dram_tensor(n, list(s), dt, kind="ExternalInput").ap()
    dout = lambda n, s, dt=F32: nc.dram_tensor(n, list(s), dt, kind="ExternalOutput").ap()
    dscr = lambda n, s, dt: nc.dram_tensor(n, list(s), dt).ap()
    xT_p = din("xT_p", [D, SEQ]); x_p = din("x_p", [SEQ, D])
    xT_s = din("xT_s", [D, 64]); x_s = din("x_s", [64, D])
    st_shift = din("st_shift", [3200]); st_wkvT = din("st_wkvT", [16, 64, 64])
    if do_l1:
        kcT = din("kcT", [16, 128, PAST]); vc = din("vc", [PAST, 2048])
    w_in = din("w_in", [D, PAB]); w_out = din("w_out", [D, D]); cw_in = din("cw_in", [D, 8192]); cw_out = din("cw_out", [D, D])
    a_ln_g = din("a_ln_g", [1, 1024]); a_ln_b = din("a_ln_b", [1, 1024])
    a_wsT = din("a_wsT", [4, 128, 128]); a_bs = din("a_bs", [1, 512])
    b_mu = din("b_mu", [3200]); b_w0 = din("b_w0", [1024]); b_w2 = din("b_w2", [64, 1024]); b_a0 = din("b_a0", [1024])
    b_a2 = din("b_a2", [64, 1024]); b_kk = din("b_kk", [1024]); b_ka = din("b_ka", [1024]); b_rk = din("b_rk", [1024])
    b_lnx_g = din("b_lnx_g", [1, 1024]); b_lnx_b = din("b_lnx_b", [1, 1024])
    ln_g = din("ln_g", [1, D]); ln_b = din("ln_b", [1, D]); cln_g = din("cln_g", [1, D]); cln_b = din("cln_b", [1, D])
    lamv = din("lamv", [1, 512]); subln_g = din("subln_g", [1, 256])
    mask2 = din("mask2", [128, 256]); sel_in = din("sel", [128, 2])
    y_p = dout("y_p", [SEQ if not do_l1 else SEQ // 2, D]); y_s = dout("y_s", [64, D])
    sh_p = dout("sh_p", [3200]); wkvT_p = dout("wkvT_p", [16, 64, 64]); sh_s = dout("sh_s", [3200]); wkvT_s = dout("wkvT_s", [16, 64, 64])
    va_s = dout("va_s", [64, 1024])
    k_p = dout("k_p", [SEQ, D]); v_p = dout("v_p", [SEQ, D]); k_s = dout("k_s", [64, D]); v_s = dout("v_s", [64, D])
    Wb_in = dscr("Wb_in", [29, 128, 16, 256], BF16); Wb_out = dscr("Wb_out", [8, 128, 16, 256], BF16)
    Wc_in = dscr("Wc_in", [32, 128, 16, 256], BF16); Wc_out = dscr("Wc_out", [8, 128, 16, 256], BF16)
    x1_all = dscr("x1_all", [SEQ + 64, D], F32); x1T_all = dscr("x1T_all", [D, SEQ + 64], BF16)
    KT = dscr("KT", [16, 128, SEQ + 64], BF16); Vb = dscr("Vb", [SEQ + 64, D], BF16)


    with ExitStack() as st:
        sb = lambda n, s, d=F32: st.enter_context(nc.sbuf_tensor(n, list(s), d))
        psb = lambda n, s, d=F32: st.enter_context(nc.psum_tensor(n, list(s), d))
        for (src, dst, nm) in ((w_in, Wb_in, "Wb_in"), (w_out, Wb_out, "Wb_out"), (cw_in, Wc_in, "Wc_in"), (cw_out, Wc_out, "Wc_out")):
            nbk = src.shape[1] // 256
            for a in range(16):
                S.dma(dst[0:nbk, :, a, :].rearrange("b p c -> p b c"), src[a * 128:(a + 1) * 128, 0:nbk * 256].rearrange("p (b c) -> p b c", c=256), writes=[(nm, a)], q="pool")
                if src.shape[1] % 256:
                    S.dma(dst[nbk, :, a, 0:128], src[a * 128:(a + 1) * 128, nbk * 256:nbk * 256 + 128], writes=[(nm, a, "t")], q="pool")
        WRES = lambda nm: [(nm, a) for a in range(16)] + ([(nm, a, "t") for a in range(16)] if nm == "Wb_in" else [])
        idf = sb("idf", [128, 128]); idb = sb("idb", [128, 128], BF16)
        mT2 = sb("mT2", [128, 128]); mL = sb("mL", [128, 64]); tri = sb("tri", [128, 128]); bones = sb("bones", [128, 128])
        hind = sb("hind", [128, 2])
        S.pool(lambda e: e.memset(idf[:], 1.0), writes=["idf"])
        S.pool(lambda e: e.affine_select(out=idf[:], in_=idf[:], pattern=[[-1, 128]], compare_op=ALU.is_equal, fill=0.0, base=0, channel_multiplier=1), reads=["idf"], writes=["idf"])
        S.dve(lambda e: e.tensor_copy(out=idb[:], in_=idf[:]), reads=["idf"], writes=["idb"])
        S.pool(lambda e: e.memset(mT2[:], 1.0), writes=["mT2"])
        S.pool(lambda e: e.memset(mL[:], 1.0), writes=["mL"])
        for hf in range(1):
            r0 = hf * 64
            S.pool(lambda e, r0=r0: e.affine_select(out=mT2[r0:r0 + 64, 0:64], in_=mT2[r0:r0 + 64, 0:64], pattern=[[1, 64]], compare_op=ALU.is_gt, fill=0.0, base=r0, channel_multiplier=-1), reads=["mT2"], writes=["mT2"])
            S.pool(lambda e, r0=r0: e.affine_select(out=mT2[r0:r0 + 64, 64:128], in_=mT2[r0:r0 + 64, 64:128], pattern=[[1, 64]], compare_op=ALU.is_ge, fill=0.0, base=r0, channel_multiplier=-1), reads=["mT2"], writes=["mT2"])
            S.pool(lambda e, r0=r0: e.affine_select(out=mL[r0:r0 + 64, :], in_=mL[r0:r0 + 64, :], pattern=[[-1, 64]], compare_op=ALU.is_gt, fill=0.0, base=-r0, channel_multiplier=1), reads=["mL"], writes=["mL"])
        S.dma(mT2[64:128, :], mT2[0:64, :], reads=["mT2"], writes=["mT2"])
        S.dma(mL[64:128, :], mL[0:64, :], reads=["mL"], writes=["mL"])
        S.pool(lambda e: e.memset(tri[:], 1.0), writes=["tri"])
        S.pool(lambda e: e.affine_select(out=tri[:], in_=tri[:], pattern=[[1, 128]], compare_op=ALU.is_ge, fill=0.0, base=0, channel_multiplier=-1), reads=["tri"], writes=["tri"])
        S.pool(lambda e: e.memset(tri[0:64, 64:128], 0.0), reads=["tri"], writes=["tri"])
        S.pool(lambda e: e.memset(bones[:], 0.0), writes=["bones"])
        S.pool(lambda e: e.memset(bones[0:64, 0:64], 1.0), reads=["bones"], writes=["bones"])
        S.pool(lambda e: e.memset(bones[64:128, 64:128], 1.0), reads=["bones"], writes=["bones"])
        S.pool(lambda e: e.memset(hind[:], 0.0), writes=["hind"])
        S.pool(lambda e: e.memset(hind[0:64, 0:1], 1.0), reads=["hind"], writes=["hind"])
        S.pool(lambda e: e.memset(hind[64:128, 1:2], 1.0), reads=["hind"], writes=["hind"])
        gb0 = sb("gb0", [128, D]); gb1 = sb("gb1", [128, D])
        wsT = gb0[:, 0:512].rearrange("p (g t) -> p g t", t=128); wsTb = sb("wsTb", [128, 4, 128], BF16); bias8 = sb("bias8", [128, 8, 128])
        S.dma(wsT, a_wsT.rearrange("g s t -> s g t"), writes=["gb0"])
        S.pool(lambda e: e.affine_select(out=wsT, in_=wsT, pattern=[[0, 4], [1, 128]], compare_op=ALU.is_ge, fill=0.0, base=0, channel_multiplier=-1), reads=["gb0"], writes=["gb0"])
        S.dve(lambda e: e.tensor_copy(out=wsTb[:], in_=wsT), reads=["gb0"], writes=["wsTb"])
        for g in range(4):
            for r in range(2):
                S.dma(bias8[:, 2 * g + r, :], a_bs[:, g * 128:(g + 1) * 128].partition_broadcast(128), writes=[("bias8", 2 * g + r)])
        BIAS8 = [("bias8", i) for i in range(8)]
        biasT = sb("biasT", [128, 4])
        S.dma(biasT[:], a_bs.rearrange("o (g t) -> t (o g)", g=4), writes=["biasT"], allow_slow_non_contiguous=True)
        def load_gb(gsrc, bsrc, w):
            S.dma(gb0[:, 0:w], gsrc.partition_broadcast(128), writes=["gb0"])
            S.dma(gb1[:, 0:w], bsrc.partition_broadcast(128), writes=["gb1"])
        alg = lxg = lng = gb0; alb = lxb = lnb = gb1
        mu25 = sb("mu25", [128, 25]); cols = sb("cols", [128, 6, 8]); w2a2 = sb("w2a2", [128, 1024])
        S.dma(mu25[:], b_mu.rearrange("(a p) -> p a", p=128), writes=["mu25"], allow_slow_non_contiguous=True)
        for i, v_ in enumerate((b_w0, b_a0, b_kk, b_ka, b_rk)):
            S.dma(cols[:, i, :], v_.rearrange("(a p) -> p a", p=128), writes=[("cols", i)], allow_slow_non_contiguous=True)
        S.dve(lambda e: e.tensor_scalar(out=cols[:, 5, :], in0=cols[:, 3, :], scalar1=-1.0, scalar2=1.0, op0=ALU.mult, op1=ALU.add), reads=[("cols", 3)], writes=[("cols", 5)])
        COLS = [("cols", i) for i in range(6)]
        S.dma(w2a2[0:64, :], b_w2, writes=["w2a2a"]); S.dma(w2a2[64:128, :], b_a2, writes=["w2a2b"])
        xT = [sb("xT%d" % i, [128, 16, 128], BF16) for i in range(1)]
        wb = [sb("wb%d" % i, [128, 16, 256], BF16) for i in range(3)]
        wb_rr = [0]
        ps = [psb("ps%d" % i, [128, 512]) for i in range(6)]
        psT = psb("psT", [128, 512]); psTb = psb("psTb", [128, 1024], BF16)
        ps_rr = [0]
        big = [sb("big%d" % i, [128, 2048]) for i in range(2)]
        oT = sb("oT", [128, 16, 128], BF16)
        vn = sb("vn", [128, 1024]); vnb = sb("vnb", [128, 1024], BF16)
        sgb = sb("sgb", [128, 8, 128]); sil = sb("sil", [128, 128]); tmpa = sb("tmpa", [128, 128])
        st8 = sb("st8", [128, 16])
        hcur2 = sb("hcur2", [128, 3, 129]); hs2 = sb("hs2", [128, 3, 128]); sig2 = sb("sig2", [128, 128]); a_t2 = sb("a_t2", [128, 128]); kkn2 = sb("kkn2", [128, 128]); tb2 = [sb("tbb%d" % i, [128, 128]) for i in range(4)]
        hcur = sb("hcur", [128, 3, 129]); hs = sb("hs", [128, 3, 128]); lora = sb("lora", [128, 128]); h24 = sb("h24", [128, 129])
        shl = [sb("shl%d" % i, [128, 25]) for i in range(2)]
        sig = sb("sig", [128, 128]); a_t = sb("a_t", [128, 128]); kkn = sb("kkn", [128, 128]); tb = [sb("tb%d" % i, [128, 128]) for i in range(4)]
        winc = sb("winc", [128, 8, 128]); ar = sb("ar", [128, 8, 2, 128]); bT = sb("bT", [128, 8, 128]); kTt = sb("kTt", [128, 8, 128])
        btok = sb("btok", [128, 1024]); ktok = sb("ktok", [128, 1024]); vtok = sb("vtok", [128, 1024]); ytok = sb("ytok", [128, 1024]); bv = sb("bv", [128, 1024])
        bsum = sb("bsum", [128, 16]); St = [sb("St%d" % i, [128, 8, 64]) for i in range(2)]
        AB4 = sb("AB4", [128, 4, 4, 128]); AK4 = sb("AK4", [128, 4, 4, 128]); Np = sb("Np", [128, 4, 2, 4, 64]); NTp = sb("NTp", [128, 4, 6, 4, 64])
        X4 = sb("X4", [128, 4, 4, 64]); t1s = sb("t1s", [128, 64])
        x1b = sb("x1b", [128, D], BF16); x1T = sb("x1T", [128, 16, 128], BF16)

        MARKS = []
        DBGS = {}
        def DBG(name, ap, reads):
            import os
            if not os.environ.get("K_DBG") or name in DBGS: return
            shp = list(ap.shape)
            t_ = nc.dram_tensor("dbg_" + name, shp, F32, kind="ExternalOutput").ap()
            DBGS[name] = t_
            S.dma(t_, ap, reads=reads)
        def MARK(n):
            MARKS.append((n, len(S.ops)))
        def wload(W, nm, c0, n):
            s_ = wb_rr[0] % len(wb); wb_rr[0] += 1
            assert c0 % 256 == 0
            S.dma(wb[s_][:, :, :], W[c0 // 256], reads=WRES(nm), writes=[("wb", s_)])
            return s_

        def nps():
            i = ps_rr[0] % len(ps); ps_rr[0] += 1
            return i

        def mm(out, lhsT, rhs, start, stop, reads, writes, tp=None):
            if tp is None:
                S.pe(lambda e: e.matmul(out, lhsT=lhsT, rhs=rhs, start=start, stop=stop), reads, writes)
            else:
                S.pe(lambda e: e.matmul(out, lhsT=lhsT, rhs=rhs, start=start, stop=stop, tile_position=tp), reads, writes)

        def ln_rows(src_ap, width, T, dst_ap, gt, bt_, eps, rsrc, rdst, gn, bn):
            junk = x1b
            S.dve(lambda e: e.memset(st8[0:T, 0:2], 0.0), writes=["st8a", "st8b"])
            S.act(lambda e: e.activation(out=junk[0:T, 0:width], in_=src_ap, func=AF.Identity, accum_out=st8[0:T, 0:1]), reads=rsrc + ["st8a"], writes=["x1b", "st8a"])
            S.act(lambda e: e.activation(out=junk[0:T, 0:width], in_=src_ap, func=AF.Square, accum_out=st8[0:T, 1:2]), reads=rsrc + ["x1b", "st8b"], writes=["x1b", "st8b"])
            S.dve(lambda e: e.tensor_scalar(out=st8[0:T, 2:3], in0=st8[0:T, 0:1], scalar1=1.0 / width, scalar2=None, op0=ALU.mult), reads=["st8a"], writes=["st8c"])
            S.dve(lambda e: e.tensor_tensor(out=st8[0:T, 3:4], in0=st8[0:T, 2:3], in1=st8[0:T, 2:3], op=ALU.mult), reads=["st8c"], writes=["st8d"])
            S.dve(lambda e: e.scalar_tensor_tensor(out=st8[0:T, 4:5], in0=st8[0:T, 1:2], scalar=1.0 / width, in1=st8[0:T, 3:4], op0=ALU.mult, op1=ALU.subtract), reads=["st8b", "st8d"], writes=["st8e"])
            S.dve(lambda e: e.tensor_scalar(out=st8[0:T, 4:5], in0=st8[0:T, 4:5], scalar1=eps, scalar2=None, op0=ALU.add), reads=["st8e"], writes=["st8e"])
            S.act(lambda e: e.activation(out=st8[0:T, 5:6], in_=st8[0:T, 4:5], func=AF.Sqrt), reads=["st8e"], writes=["st8f"])
            S.dve(lambda e: e.reciprocal(out=st8[0:T, 6:7], in_=st8[0:T, 5:6]), reads=["st8f"], writes=["st8g"])
            S.dve(lambda e: e.scalar_tensor_tensor(out=st8[0:T, 7:8], in0=st8[0:T, 2:3], scalar=-1.0, in1=st8[0:T, 6:7], op0=ALU.mult, op1=ALU.mult), reads=["st8c", "st8g"], writes=["st8h"])
            S.act(lambda e: e.activation(out=dst_ap, in_=src_ap, func=AF.Identity, bias=st8[0:T, 7:8], scale=st8[0:T, 6:7]), reads=rsrc + ["st8g", "st8h"], writes=rdst)
            S.dve(lambda e: e.tensor_tensor(out=dst_ap, in0=dst_ap, in1=gt[0:T, 0:width], op=ALU.mult), reads=rdst + ["gb0"], writes=rdst)
            S.dve(lambda e: e.tensor_tensor(out=dst_ap, in0=dst_ap, in1=bt_[0:T, 0:width], op=ALU.add), reads=rdst + ["gb1"], writes=rdst)

        def l0_tile(T, xT_src, x_src, xs, sti, first, last, outs, row0):
            xt = xT[xs]; XR = ("xT", xs)
            S.dma(xt[:, :, 0:T], xT_src.rearrange("(a p) t -> p a t", p=128), writes=[XR], q="pool")
            nch = T // 64
            STR = ("St", sti); SHR = ("shl", sti)
            stt = St[sti]; shlast = shl[sti]

            def lin_cm(pi, s_, c0, reads_w):
                for a in range(16):
                    mm(ps[pi][:, 0:T], wb[s_][:, a, c0:c0 + 128], xt[:, a, 0:T], a == 0, a == 15, [XR, ("wb", s_)], [("ps", pi)])

            load_gb(a_ln_g, a_ln_b, 1024)
            pa, pb = nps(), nps()
            for q4 in range(4):
                s_ = wload(Wb_in, "Wb_in", 1024 + q4 * 256, 256); pi = (pa, pb)[q4 // 2]
                for a in range(16):
                    mm(ps[pi][0:T, (q4 % 2) * 256:(q4 % 2) * 256 + 256], xt[:, a, 0:T], wb[s_][:, a, :], a == 0, a == 15, [XR, ("wb", s_)], [("ps", pi)])
            S.act(lambda e: e.copy(out=big[0][0:T, 0:512], in_=ps[pa][0:T, :]), reads=[("ps", pa)], writes=["big0a"])
            S.dve(lambda e: e.tensor_copy(out=big[0][0:T, 512:1024], in_=ps[pb][0:T, :]), reads=[("ps", pb)], writes=["big0b"])
            ln_rows(big[0][0:T, 0:1024], 1024, T, vn[0:T, :], alg, alb, 1e-5, ["big0a", "big0b"], ["vn"], "alg", "alb")
            S.act(lambda e: e.copy(out=vnb[0:T, :], in_=vn[0:T, :]), reads=["vn"], writes=["vnb"])
            if outs.get("va") is not None:
                S.dma(outs["va"], vn[0:T, :], reads=["vn"])
            sgt = sgb[:].rearrange("p a t -> p (a t)")
            psg = [nps(), nps()]
            for g in range(4):
                pi = psg[g // 2]; cs_ = slice((g % 2) * 256, (g % 2) * 256 + 256)
                mm(ps[pi][0:T, cs_], wsTb[0:T, g, 0:T], vnb[0:T, g * 256:(g + 1) * 256], True, True, ["vnb", "wsTb"], [("ps", pi)])
            for g in range(4):
                pi = psg[g // 2]; cs_ = slice((g % 2) * 256, (g % 2) * 256 + 256)
                S.dve(lambda e, pi=pi, cs_=cs_, g=g: e.tensor_scalar(out=sgt[0:T, g * 256:(g + 1) * 256], in0=ps[pi][0:T, cs_], scalar1=biasT[0:T, g:g + 1], scalar2=None, op0=ALU.add), reads=[("ps", pi), "biasT"], writes=[("sgb", g)])
            for blk in range(4):
                su = wload(Wb_in, "Wb_in", blk * 256, 256); sg_ = wload(Wb_in, "Wb_in", 2048 + blk * 256, 256)
                pu, pg = nps(), nps()
                for (pi, s_) in ((pu, su), (pg, sg_)):
                    for a in range(16):
                        mm(ps[pi][0:T, 0:256], xt[:, a, 0:T], wb[s_][:, a, :], a == 0, a == 15, [XR, ("wb", s_)], [("ps", pi)])
                bs_ = slice(blk * 256, (blk + 1) * 256)
                S.act(lambda e, pg=pg, bs_=bs_: e.activation(out=big[0][0:T, bs_], in_=ps[pg][0:T, 0:256], func=AF.Silu), reads=[("ps", pg), "big0a", "big0b"], writes=["big0a", "big0b"])
                S.dve(lambda e, pu=pu, bs_=bs_: e.tensor_tensor(out=big[1][0:T, bs_], in0=ps[pu][0:T, 0:256], in1=sgt[0:T, bs_], op=ALU.mult), reads=[("ps", pu), ("sgb", blk), "big1"], writes=["big1"])
                S.dve(lambda e, bs_=bs_: e.tensor_tensor(out=x1b[0:T, bs_], in0=big[1][0:T, bs_], in1=big[0][0:T, bs_], op=ALU.mult), reads=["big1", "big0a", "big0b", "x1b"], writes=["x1b"])
            for cc in range(8):
                S.pe(lambda e, cc=cc: e.transpose(psTb[:, cc * 128:cc * 128 + T], x1b[0:T, cc * 128:(cc + 1) * 128], idb[0:T, 0:T]), reads=["x1b", "idb"], writes=["psTb"])
            S.act(lambda e: e.copy(out=oT[:, 0:8, 0:T], in_=psTb[:].rearrange("p (a t) -> p a t", t=128)[:, :, 0:T]), reads=["psTb"], writes=[("oT", i) for i in range(8)])

            s24 = wload(Wb_in, "Wb_in", 7168, 128)
            p24 = nps(); lin_cm(p24, s24, 0, None)
            if first:
                S.dma(shlast[:], outs["shift0"].rearrange("(a p) -> p a", p=128), writes=[SHR], allow_slow_non_contiguous=True) if outs.get("shift0") is not None else \
                    S.pool(lambda e: e.memset(shlast[:], 0.0), writes=[SHR])
            S.act(lambda e: e.copy(out=h24[:, 1:T + 1], in_=ps[p24][:, 0:T]), reads=[("ps", p24)], writes=["h24"])
            S.dve(lambda e: e.tensor_copy(out=h24[:, 0:1], in_=shlast[:, 24:25]), reads=[SHR], writes=["h24p"])
            S.dve(lambda e: e.tensor_tensor(out=tb[0][:, 0:T], in0=h24[:, 0:T], in1=h24[:, 1:T + 1], op=ALU.subtract), reads=["h24", "h24p"], writes=["tb0"])
            S.dve(lambda e: e.scalar_tensor_tensor(out=lora[:, 0:T], in0=tb[0][:, 0:T], scalar=mu25[:, 24:25], in1=h24[:, 1:T + 1], op0=ALU.mult, op1=ALU.add), reads=["tb0", "h24", "mu25"], writes=["lora"])
            S.dve(lambda e: e.tensor_copy(out=shlast[:, 24:25], in_=h24[:, T:T + 1]), reads=["h24", "h24p", SHR], writes=[SHR])
            S.act(lambda e: e.activation(out=lora[0:64, 0:T], in_=lora[0:64, 0:T], func=AF.Tanh), reads=["lora"], writes=["lora"])
            hcurs = [hcur, hcur2]; hss = [hs, hs2]; sigs = [sig, sig2]; a_ts = [a_t, a_t2]; kkns = [kkn, kkn2]; tbs = [tb, tb2]

            def prep(hp, sl):
                par = hp % 2
                hcur_ = hcurs[par]; hs_t = hss[par]; sig_ = sigs[par]; a_t_ = a_ts[par]; kkn_ = kkns[par]; tb_ = tbs[par]
                pis = [nps() for _ in range(3)]
                for j in range(3):
                    lin_cm(pis[j], sl[j], (hp % 2) * 128, None)
                    chn = j * 8 + hp
                    S.act(lambda e, j=j, pi=pis[j]: e.copy(out=hcur_[:, j, 1:T + 1], in_=ps[pi][:, 0:T]), reads=[("ps", pis[j])], writes=[("hcur", par, j)])
                    S.dve(lambda e, j=j, chn=chn: e.tensor_copy(out=hcur_[:, j, 0:1], in_=shlast[:, chn:chn + 1]), reads=[SHR], writes=[("hcurp", par, j)])
                    S.dve(lambda e, j=j: e.tensor_tensor(out=tb_[0][:, 0:T], in0=hcur_[:, j, 0:T], in1=hcur_[:, j, 1:T + 1], op=ALU.subtract), reads=[("hcur", par, j), ("hcurp", par, j)], writes=[("tb0", par)])
                    S.dve(lambda e, j=j, chn=chn: e.scalar_tensor_tensor(out=hs_t[:, j, 0:T], in0=tb_[0][:, 0:T], scalar=mu25[:, chn:chn + 1], in1=hcur_[:, j, 1:T + 1], op0=ALU.mult, op1=ALU.add),
                          reads=[("tb0", par), ("hcur", par, j), "mu25"], writes=[("hs", par, j)])
                    S.dve(lambda e, j=j, chn=chn: e.tensor_copy(out=shlast[:, chn:chn + 1], in_=hcur_[:, j, T:T + 1]), reads=[("hcur", par, j), ("hcurp", par, j), SHR], writes=[SHR])
                yield
                rs_, ks_, vs_ = hs_t[:, 0, 0:T], hs_t[:, 1, 0:T], hs_t[:, 2, 0:T]
                pz = nps()
                mm(ps[pz][:, 0:T], w2a2[0:64, hp * 128:(hp + 1) * 128], lora[0:64, 0:T], True, True, ["w2a2a", "lora"], [("ps", pz)])
                S.act(lambda e, pz=pz, hp=hp: e.activation(out=sig_[:, 0:T], in_=ps[pz][:, 0:T], func=AF.Sigmoid, bias=cols[:, 0, hp:hp + 1], scale=1.0), reads=[("ps", pz)] + COLS, writes=[("sig", par)])
                pq = nps()
                mm(ps[pq][:, 0:T], w2a2[64:128, hp * 128:(hp + 1) * 128], lora[64:128, 0:T], True, True, ["w2a2b", "lora"], [("ps", pq)], tp=(64, 0))
                S.act(lambda e, pq=pq, hp=hp: e.activation(out=a_t_[:, 0:T], in_=ps[pq][:, 0:T], func=AF.Sigmoid, bias=cols[:, 1, hp:hp + 1], scale=1.0), reads=[("ps", pq)] + COLS, writes=[("a_t", par)])
                yield
                S.dve(lambda e, hp=hp: e.tensor_scalar(out=tb_[1][:, 0:T], in0=ks_, scalar1=cols[:, 2, hp:hp + 1], scalar2=None, op0=ALU.mult), reads=[("hs", par, 1)] + COLS, writes=[("tb1", par)])
                S.dve(lambda e: e.tensor_tensor(out=tb_[2][:, 0:T], in0=tb_[1][:, 0:T], in1=tb_[1][:, 0:T], op=ALU.mult), reads=[("tb1", par)], writes=[("tb2", par)])
                pn = nps()
                mm(ps[pn][:, 0:T], bones[:], tb_[2][:, 0:T], True, True, ["bones", ("tb2", par)], [("ps", pn)])
                yield
                S.act(lambda e, pn=pn: e.activation(out=tb_[2][:, 0:T], in_=ps[pn][:, 0:T], func=AF.Sqrt), reads=[("ps", pn)], writes=[("tb2", par)])
                S.dve(lambda e: e.tensor_scalar(out=tb_[2][:, 0:T], in0=tb_[2][:, 0:T], scalar1=1e-12, scalar2=None, op0=ALU.max), reads=[("tb2", par)], writes=[("tb2", par)])
                S.dve(lambda e: e.reciprocal(out=tb_[2][:, 0:T], in_=tb_[2][:, 0:T]), reads=[("tb2", par)], writes=[("tb2", par)])
                S.dve(lambda e: e.tensor_tensor(out=kkn_[:, 0:T], in0=tb_[1][:, 0:T], in1=tb_[2][:, 0:T], op=ALU.mult), reads=[("tb1", par), ("tb2", par)], writes=[("kkn", par)])
                yield
                S.dve(lambda e, hp=hp: e.tensor_scalar(out=tb_[1][:, 0:T], in0=a_t_[:, 0:T], scalar1=cols[:, 3, hp:hp + 1], scalar2=cols[:, 5, hp:hp + 1], op0=ALU.mult, op1=ALU.add), reads=[("a_t", par), ("tb1", par)] + COLS, writes=[("tb1", par)])
                S.dve(lambda e: e.tensor_tensor(out=tb_[1][:, 0:T], in0=tb_[1][:, 0:T], in1=ks_, op=ALU.mult), reads=[("tb1", par), ("hs", par, 1)], writes=[("tb1", par)])
                S.dve(lambda e: e.tensor_tensor(out=tb_[2][:, 0:T], in0=kkn_[:, 0:T], in1=a_t_[:, 0:T], op=ALU.mult), reads=[("kkn", par), ("a_t", par)], writes=[("tb2", par)])
                S.dve(lambda e, hp=hp: e.scalar_tensor_tensor(out=tb_[3][:, 0:T], in0=rs_, scalar=cols[:, 4, hp:hp + 1], in1=tb_[1][:, 0:T], op0=ALU.mult, op1=ALU.mult), reads=[("hs", par, 0), ("tb1", par)] + COLS, writes=[("tb3", par)])
                mm(psT[0:T, 256 + 2 * hp:256 + 2 * hp + 2], tb_[3][:, 0:T], hind[:], True, True, [("tb3", par), "hind"], ["psTbn"])
                yield
                S.pe(lambda e: e.transpose(psT[0:T, 0:128], sig_[:, 0:T], idf[:]), reads=[("sig", par), "idf"], writes=["psT"])
                S.act(lambda e: e.activation(out=tb_[3][0:T, :], in_=psT[0:T, 0:128], func=AF.Copy, scale=-C_DEC), reads=["psT", ("tb3", par)], writes=[("tb3", par)])
                pc = nps()
                mm(ps[pc][:, 0:T], tb_[3][0:T, :], tri[0:T, 0:T], True, True, [("tb3", par), "tri"], [("ps", pc)])
                yield
                S.act(lambda e, pc=pc, hp=hp: e.activation(out=winc[:, hp, 0:T], in_=ps[pc][:, 0:T], func=AF.Exp), reads=[("ps", pc)], writes=[("winc", hp)])
                S.act(lambda e, pc=pc: e.activation(out=tb_[3][:, 0:T], in_=ps[pc][:, 0:T], func=AF.Exp, scale=-1.0), reads=[("ps", pc), ("tb3", par)], writes=[("tb3", par)])
                S.dve(lambda e, pc=pc: e.scalar_tensor_tensor(out=sig_[:, 0:T], in0=sig_[:, 0:T], scalar=C_DEC, in1=ps[pc][:, 0:T], op0=ALU.mult, op1=ALU.add), reads=[("sig", par), ("ps", pc), "psT"], writes=[("sig", par)])
                S.act(lambda e: e.activation(out=sig_[:, 0:T], in_=sig_[:, 0:T], func=AF.Exp), reads=[("sig", par)], writes=[("sig", par)])
                yield
                arv = ar[:, hp].rearrange("p c (w t) -> p c w t", w=2)
                T3 = lambda ap: ap.rearrange("p (c t) -> p c t", t=64)
                S.dve(lambda e, hp=hp, arv=arv: e.tensor_tensor(out=arv[:, 0:nch, 1, :], in0=T3(rs_), in1=T3(winc[:, hp, 0:T]), op=ALU.mult), reads=[("hs", par, 0), ("winc", hp)], writes=[("ar", hp, 1)])
                S.dve(lambda e, hp=hp, arv=arv: e.scalar_tensor_tensor(out=arv[:, 0:nch, 0, :], in0=T3(kkn_[:, 0:T]), scalar=-1.0, in1=T3(sig_[:, 0:T]), op0=ALU.mult, op1=ALU.mult), reads=[("kkn", par), ("sig", par)], writes=[("ar", hp, 0)])
                S.dve(lambda e, hp=hp: e.tensor_tensor(out=bT[:, hp, 0:T], in0=tb_[2][:, 0:T], in1=tb_[3][:, 0:T], op=ALU.mult), reads=[("tb2", par), ("tb3", par)], writes=[("bT", hp)])
                S.dve(lambda e, hp=hp: e.tensor_tensor(out=kTt[:, hp, 0:T], in0=tb_[1][:, 0:T], in1=tb_[3][:, 0:T], op=ALU.mult), reads=[("tb1", par), ("tb3", par)], writes=[("kT", hp)])
                yield
                for (src_ap, dstt, rn, wn) in ((bT[:, hp, 0:T], btok, ("bT", hp), "btok"), (kTt[:, hp, 0:T], ktok, ("kT", hp), "ktok"), (vs_, vtok, ("hs", par, 2), "vtok")):
                    S.pe(lambda e, src_ap=src_ap: e.transpose(psT[0:T, 0:128], src_ap, idf[:]), reads=[rn, "idf"], writes=["psT"])
                    S.act(lambda e, dstt=dstt, hp=hp: e.copy(out=dstt[0:T, hp * 128:(hp + 1) * 128], in_=psT[0:T, 0:128]), reads=["psT"], writes=[(wn, hp)])
            for hp0 in range(0, 8, 2):
                sl_ = [wload(Wb_in, "Wb_in", 3072 + j * 1024 + (hp0 // 2) * 256, 256) for j in range(3)]
                gens = [prep(hp0, sl_), prep(hp0 + 1, sl_)]
                while gens:
                    for g_ in list(gens):
                        try:
                            next(g_)
                        except StopIteration:
                            gens.remove(g_)
            S.act(lambda e: e.copy(out=bsum[0:T, :], in_=psT[0:T, 256:272]), reads=["psTbn"], writes=["bsum"])
            V3 = lambda ap: ap.rearrange("p (h v) -> p h v", v=64)
            S.pool(lambda e: e.tensor_tensor(out=V3(bv[0:T, :]), in0=V3(vtok[0:T, :]), in1=bsum[0:T, :].unsqueeze(2).to_broadcast([T, 16, 64]), op=ALU.mult),
                   reads=["bsum"] + [("vtok", h) for h in range(8)], writes=["bv"])
            if first:
                if outs.get("wkv0") is not None:
                    S.dma(stt[:], outs["wkv0"].rearrange("(a h) k v -> (h k) a v", h=2), writes=[STR])
                else:
                    S.pool(lambda e: e.memset(stt[:], 0.0), writes=[STR])
            def hd(g, hl):
                hp = 2 * g + hl // 2; kp = 64 * (hl % 2)
                return hp, kp, slice(kp, kp + 64)
            TP = 64 * nch; psl = slice(0, TP)
            MARK("S1")
            for g in range(4):
                m2 = mT2[psl, :].unsqueeze(1).to_broadcast([TP, 2, 128]); m1 = mL[psl, :].unsqueeze(1).to_broadcast([TP, 2, 64])
                for h2 in range(2):
                    pA, pB, pC = nps(), nps(), nps()
                    for j in range(2):
                        hl = 2 * j + h2
                        hp, kp, ksl = hd(g, hl)
                        for c in range(nch):
                            tp_ = 64 * c; tsl = slice(tp_, tp_ + 64)
                            mm(ps[pA][tsl, j * 128:(j + 1) * 128], bT[ksl, hp, tp_:tp_ + 64], ar[ksl, hp, c, :], True, True, [("bT", hp), ("ar", hp, 0), ("ar", hp, 1)], [("ps", pA)], tp=(kp, tp_))
                            mm(ps[pB][tsl, j * 128:(j + 1) * 128], kTt[ksl, hp, tp_:tp_ + 64], ar[ksl, hp, c, :], True, True, [("kT", hp), ("ar", hp, 0), ("ar", hp, 1)], [("ps", pB)], tp=(kp, tp_))
                            mm(ps[pC][tsl, j * 64:(j + 1) * 64], ar[ksl, hp, c, 0:64], bT[ksl, hp, tp_:tp_ + 64], True, True, [("bT", hp), ("ar", hp, 0)], [("ps", pC)], tp=(kp, tp_))
                    S.dve(lambda e, g=g, pA=pA, m2=m2, h2=h2: e.tensor_tensor(out=AB4[psl, g, h2::2, :], in0=ps[pA][psl, 0:256].rearrange("p (h t) -> p h t", t=128), in1=m2, op=ALU.mult), reads=[("ps", pA), "mT2"], writes=[("AB4", g)])
                    S.dve(lambda e, g=g, pB=pB, m2=m2, h2=h2: e.tensor_tensor(out=AK4[psl, g, h2::2, :], in0=ps[pB][psl, 0:256].rearrange("p (h t) -> p h t", t=128), in1=m2, op=ALU.mult), reads=[("ps", pB), "mT2"], writes=[("AK4", g)])
                    S.dve(lambda e, g=g, pC=pC, m1=m1, h2=h2: e.tensor_tensor(out=Np[psl, g, 0, h2::2, :], in0=ps[pC][psl, 0:128].rearrange("p (h t) -> p h t", t=64), in1=m1, op=ALU.mult), reads=[("ps", pC), "mL"], writes=[("Np", g)])
                S.pool(lambda e, g=g: e.tensor_copy(out=NTp[psl, g, 0], in_=AB4[psl, g, :, 0:64]), reads=[("AB4", g)], writes=[("NTp", g)])
            MARK("S3")
            for l in range(5):
                for g in range(4):
                    pP, pQ = nps(), nps()
                    for hl in range(4):
                        for c in range(nch):
                            tp_ = 64 * c; tsl = slice(tp_, tp_ + 64)
                            mm(ps[pP][tsl, hl * 64:(hl + 1) * 64], NTp[tsl, g, l, hl, :], Np[tsl, g, l % 2, hl, :], True, True, [("NTp", g), ("Np", g)], [("ps", pP)], tp=(tp_, tp_))
                            mm(ps[pQ][tsl, hl * 64:(hl + 1) * 64], Np[tsl, g, l % 2, hl, :], NTp[tsl, g, l, hl, :], True, True, [("NTp", g), ("Np", g)], [("ps", pQ)], tp=(tp_, tp_))
                    S.dve(lambda e, g=g, l=l, pP=pP: e.tensor_copy(out=Np[psl, g, (l + 1) % 2], in_=ps[pP][psl, 0:256].rearrange("p (h t) -> p h t", t=64)), reads=[("ps", pP)], writes=[("Np", g)])
                    S.act(lambda e, g=g, l=l, pQ=pQ: e.copy(out=NTp[psl, g, l + 1], in_=ps[pQ][psl, 0:256].rearrange("p (h t) -> p h t", t=64)), reads=[("ps", pQ)], writes=[("NTp", g)])
            for c in range(nch):
                tp_ = 64 * c
                tsl = slice(tp_, tp_ + 64)
                MARK("S4")
                for g in range(4):
                    pX2 = nps()
                    for h2 in range(2):
                        pX = nps()
                        for j in range(2):
                            hl = 2 * j + h2
                            hp, kp, ksl = hd(g, hl)
                            mm(ps[pX][tsl, j * 64:(j + 1) * 64], ar[ksl, hp, c, 0:64], stt[ksl, hp, :], True, True, [("ar", hp, 0), STR], [("ps", pX)], tp=(kp, tp_))
                        S.act(lambda e, tsl=tsl, g=g, pX=pX, h2=h2: e.copy(out=X4[tsl, g, h2::2, :], in_=ps[pX][tsl, 0:128].rearrange("p (h t) -> p h t", t=64)), reads=[("ps", pX)], writes=[("X4", g)])
                    for hl in range(4):
                        hp, kp, ksl = hd(g, hl)
                        mm(ps[pX2][tsl, hl * 64:(hl + 1) * 64], AK4[tsl, g, hl, 0:64], vtok[tsl, hp * 128 + kp:hp * 128 + kp + 64], True, True, [("AK4", g), ("vtok", hp)], [("ps", pX2)], tp=(tp_, tp_))
                    S.dve(lambda e, tsl=tsl, g=g, pX2=pX2: e.tensor_tensor(out=X4[tsl, g], in0=ps[pX2][tsl, 0:256].rearrange("p (h t) -> p h t", t=64), in1=X4[tsl, g], op=ALU.add), reads=[("ps", pX2), ("X4", g)], writes=[("X4", g)])
                MARK("S5")
                for l in range(5, -1, -1):
                    for g in range(4):
                        pU = nps()
                        for hl in range(4):
                            mm(ps[pU][tsl, hl * 64:(hl + 1) * 64], NTp[tsl, g, l, hl, :], X4[tsl, g, hl, :], True, True, [("NTp", g), ("X4", g)], [("ps", pU)], tp=(tp_, tp_))
                        S.dve(lambda e, tsl=tsl, g=g, pU=pU: e.tensor_tensor(out=X4[tsl, g], in0=ps[pU][tsl, 0:256].rearrange("p (h t) -> p h t", t=64), in1=X4[tsl, g], op=ALU.add), reads=[("ps", pU), ("X4", g)], writes=[("X4", g)])
                MARK("S6")
                for g in range(4):
                    pY2 = nps()
                    y4 = ytok[tsl, g * 256:(g + 1) * 256].rearrange("p (h v) -> p h v", v=64)
                    for h2 in range(2):
                        pY = nps()
                        for j in range(2):
                            hl = 2 * j + h2
                            hp, kp, ksl = hd(g, hl)
                            mm(ps[pY][tsl, j * 64:(j + 1) * 64], ar[ksl, hp, c, 64:128], stt[ksl, hp, :], True, True, [("ar", hp, 1), STR], [("ps", pY)], tp=(kp, tp_))
                        S.act(lambda e, tsl=tsl, pY=pY, h2=h2, y4=y4: e.copy(out=y4[:, h2::2, :], in_=ps[pY][tsl, 0:128].rearrange("p (h t) -> p h t", t=64)), reads=[("ps", pY)], writes=[("ytok", g, c)])
                    for hl in range(4):
                        hp, kp, ksl = hd(g, hl)
                        vsl = slice(hp * 128 + kp, hp * 128 + kp + 64)
                        mm(ps[pY2][tsl, hl * 64:(hl + 1) * 64], AB4[tsl, g, hl, 64:128], X4[tsl, g, hl, :], True, False, [("AB4", g), ("X4", g)], [("ps", pY2)], tp=(tp_, tp_))
                        mm(ps[pY2][tsl, hl * 64:(hl + 1) * 64], AK4[tsl, g, hl, 64:128], vtok[tsl, vsl], False, True, [("AK4", g), ("vtok", hp)], [("ps", pY2)], tp=(tp_, tp_))
                    S.dve(lambda e, tsl=tsl, g=g, pY2=pY2: e.tensor_tensor(out=ytok[tsl, g * 256:(g + 1) * 256], in0=ps[pY2][tsl, 0:256], in1=ytok[tsl, g * 256:(g + 1) * 256], op=ALU.add), reads=[("ps", pY2), ("ytok", g, c)], writes=[("ytok", g, c)])
                for g in range(4):
                    pS = nps()
                    for hl in range(4):
                        hp, kp, ksl = hd(g, hl)
                        vsl = slice(hp * 128 + kp, hp * 128 + kp + 64)
                        mm(ps[pS][ksl, (hl // 2) * 64:(hl // 2) * 64 + 64], btok[tsl, vsl], X4[tsl, g, hl, :], True, False, [("btok", hp), ("X4", g)], [("ps", pS)], tp=(tp_, kp))
                        mm(ps[pS][ksl, (hl // 2) * 64:(hl // 2) * 64 + 64], ktok[tsl, vsl], vtok[tsl, vsl], False, True, [("ktok", hp), ("vtok", hp)], [("ps", pS)], tp=(tp_, kp))
                    for j in range(2):
                        hp = 2 * g + j
                        wl_ = winc[:, hp, tp_ + 63:tp_ + 64]
                        S.dve(lambda e, tsl=tsl, hp=hp, wl_=wl_: e.tensor_scalar(out=t1s[:], in0=stt[:, hp, :], scalar1=wl_, scalar2=None, op0=ALU.mult), reads=[STR, ("winc", hp)], writes=["t1s"])
                        S.dve(lambda e, tsl=tsl, hp=hp, wl_=wl_, j=j, pS=pS: e.scalar_tensor_tensor(out=stt[:, hp, :], in0=ps[pS][:, j * 64:(j + 1) * 64], scalar=wl_, in1=t1s[:], op0=ALU.mult, op1=ALU.add),
                              reads=[("ps", pS), "t1s", ("winc", hp), STR], writes=[STR])
            YT = [("ytok", g, c) for g in range(4) for c in range(nch)]
            MARK("Bepi")
            load_gb(b_lnx_g, b_lnx_b, 1024)
            y3 = V3(ytok[0:T, :])
            bc = lambda col: st8[0:T, col].unsqueeze(2).to_broadcast([T, 16, 64])
            gst = big[1]
            S.dve(lambda e: e.tensor_reduce(out=gst[0:T, 0:16], in_=y3, axis=AX.X, op=ALU.add), reads=YT, writes=["big1"])
            S.dve(lambda e: e.tensor_scalar(out=gst[0:T, 0:16], in0=gst[0:T, 0:16], scalar1=1.0 / 64, scalar2=None, op0=ALU.mult), reads=["big1"], writes=["big1"])
            S.dve(lambda e: e.tensor_tensor(out=y3, in0=y3, in1=gst[0:T, 0:16].unsqueeze(2).to_broadcast([T, 16, 64]), op=ALU.subtract), reads=YT + ["big1"], writes=["ytokc"])
            S.dve(lambda e: e.tensor_tensor(out=V3(big[0][0:T, 0:1024]), in0=y3, in1=y3, op=ALU.mult), reads=["ytokc", "big0a", "big0b"], writes=["big0a", "big0b"])
            S.dve(lambda e: e.tensor_reduce(out=gst[0:T, 16:32], in_=V3(big[0][0:T, 0:1024]), axis=AX.X, op=ALU.add), reads=["big0a", "big0b", "big1"], writes=["big1"])
            S.dve(lambda e: e.tensor_scalar(out=gst[0:T, 16:32], in0=gst[0:T, 16:32], scalar1=1.0 / 64, scalar2=64e-5, op0=ALU.mult, op1=ALU.add), reads=["big1"], writes=["big1"])
            S.act(lambda e: e.activation(out=gst[0:T, 16:32], in_=gst[0:T, 16:32], func=AF.Sqrt), reads=["big1"], writes=["big1"])
            S.dve(lambda e: e.reciprocal(out=gst[0:T, 16:32], in_=gst[0:T, 16:32]), reads=["big1"], writes=["big1"])
            S.dve(lambda e: e.tensor_tensor(out=y3, in0=y3, in1=gst[0:T, 16:32].unsqueeze(2).to_broadcast([T, 16, 64]), op=ALU.mult), reads=["ytokc", "big1"], writes=["ytokc"])
            S.dve(lambda e: e.tensor_tensor(out=ytok[0:T, :], in0=ytok[0:T, :], in1=lxg[0:T, 0:1024], op=ALU.mult), reads=["ytokc", "gb0"], writes=["ytokc"])
            S.dve(lambda e: e.tensor_tensor(out=ytok[0:T, :], in0=ytok[0:T, :], in1=lxb[0:T, 0:1024], op=ALU.add), reads=["ytokc", "gb1"], writes=["ytokc"])
            S.dve(lambda e: e.tensor_tensor(out=ytok[0:T, :], in0=ytok[0:T, :], in1=bv[0:T, :], op=ALU.add), reads=["ytokc", "bv"], writes=["ytokc"])
            for blk in range(4):
                sgw = wload(Wb_in, "Wb_in", 6144 + blk * 256, 256)
                pg = nps()
                for a in range(16):
                    mm(ps[pg][0:T, 0:256], xt[:, a, 0:T], wb[sgw][:, a, :], a == 0, a == 15, [XR, ("wb", sgw)], [("ps", pg)])
                S.act(lambda e, pg=pg, blk=blk: e.activation(out=big[0][0:T, blk * 256:(blk + 1) * 256], in_=ps[pg][0:T, 0:256], func=AF.Silu), reads=[("ps", pg), "big0a", "big0b"], writes=["big0a" if blk < 2 else "big0b"])
            S.dve(lambda e: e.tensor_tensor(out=x1b[0:T, 0:1024], in0=ytok[0:T, :], in1=big[0][0:T, 0:1024], op=ALU.mult), reads=["ytokc", "big0a", "big0b"], writes=["x1b"])
            for cc in range(8):
                S.pe(lambda e, cc=cc: e.transpose(psTb[:, cc * 128:cc * 128 + T], x1b[0:T, cc * 128:(cc + 1) * 128], idb[0:T, 0:T]), reads=["x1b", "idb"], writes=["psTb"])
            S.act(lambda e: e.copy(out=oT[:, 8:16, 0:T], in_=psTb[:].rearrange("p (a t) -> p a t", t=128)[:, :, 0:T]), reads=["psTb"], writes=[("oT", 8 + i) for i in range(8)])
            OT = [("oT", i) for i in range(16)]
            MARK("outproj")
            S.dma(big[1][0:T, :], x_src, reads=[], writes=["big1"])
            load_gb(ln_g, ln_b, D)
            for blk in range(8):
                sw = wload(Wb_out, "Wb_out", blk * 256, 256)
                po = nps()
                for a in range(16):
                    mm(ps[po][0:T, 0:256], oT[:, a, 0:T], wb[sw][:, a, :], a == 0, a == 15, OT + [("wb", sw)], [("ps", po)])
                S.dve(lambda e, po=po, blk=blk: e.scalar_tensor_tensor(out=big[1][0:T, blk * 256:(blk + 1) * 256], in0=big[1][0:T, blk * 256:(blk + 1) * 256], scalar=ALPHA, in1=ps[po][0:T, 0:256], op0=ALU.mult, op1=ALU.add),
                      reads=[("ps", po), "big1"], writes=["big1"])
            ln_rows(big[1][0:T, :], D, T, big[0][0:T, :], lng, lnb, 1e-5, ["big1"], ["big0a", "big0b"], "lng", "lnb")
            if outs.get("dbg") is not None:
                S.dma(outs["dbg"], big[0][0:T, :], reads=["big0a", "big0b"])
            S.act(lambda e: e.copy(out=x1b[0:T, :], in_=big[0][0:T, :]), reads=["big0a", "big0b"], writes=["x1b"])
            for half in range(2):
                for cc in range(8):
                    S.pe(lambda e, cc=cc, half=half: e.transpose(psTb[:, cc * 128:cc * 128 + T], x1b[0:T, (half * 8 + cc) * 128:(half * 8 + cc + 1) * 128], idb[0:T, 0:T]), reads=["x1b", "idb"], writes=["psTb"])
                S.dve(lambda e, half=half: e.tensor_copy(out=x1T[:, half * 8:half * 8 + 8, 0:T], in_=psTb[:].rearrange("p (a t) -> p a t", t=128)[:, :, 0:T]), reads=["psTb"], writes=[("x1T", half)])
            S.dma(x1T_all[:, row0:row0 + T].rearrange("(a p) t -> p a t", p=128), x1T[:, :, 0:T], reads=[("x1T", 0), ("x1T", 1)], writes=[("x1T_all", row0 // 128)])
            S.dma(x1_all[row0:row0 + T, :], big[0][0:T, :], reads=["big0a", "big0b"], writes=[("x1_all", row0 // 128)])
            if last:
                S.dma(outs["sh"].rearrange("(a p) -> p a", p=128), shlast[:], reads=[SHR], allow_slow_non_contiguous=True)
                S.dma(outs["wkvT"].rearrange("(a h) k v -> (h k) a v", h=2), stt[:], reads=[STR])

        for i in range(NT):
            l0_tile(128, xT_p[:, i * 128:(i + 1) * 128], x_p[i * 128:(i + 1) * 128, :], 0, 0, i == 0, i == NT - 1,
                    {"sh": sh_p, "wkvT": wkvT_p, "dbg": (y_p[i * 128:(i + 1) * 128, :] if not do_l1 else None)}, i * 128)
        l0_tile(64, xT_s, x_s, 0, 1, True, True, {"sh": sh_s, "wkvT": wkvT_s, "va": va_s, "shift0": st_shift, "wkv0": st_wkvT,
                                                   "dbg": (y_s if not do_l1 else None)}, SEQ)
        print("SBUF remaining", nc.sbuf_bytes_remaining)
        if not do_l1:
            info = S.emit()
    if do_l1:
        S.barrier()
        with ExitStack() as st:
            sb = lambda n, s, d=F32: st.enter_context(nc.sbuf_tensor("L1" + n, list(s), d))
            psb = lambda n, s, d=F32: st.enter_context(nc.psum_tensor("L1" + n, list(s), d))
            idf_2 = sb("idf", [128, 128]); idb_2 = sb("idb", [128, 128], BF16)
            S.pool(lambda e: e.memset(idf_2[:], 1.0), writes=["idf"])
            S.pool(lambda e: e.affine_select(out=idf_2[:], in_=idf_2[:], pattern=[[-1, 128]], compare_op=ALU.is_equal, fill=0.0, base=0, channel_multiplier=1), reads=["idf"], writes=["idf"])
            S.dve(lambda e: e.tensor_copy(out=idb_2[:], in_=idf_2[:]), reads=["idf"], writes=["idb"])
            gb0_2 = sb("gb0", [128, D]); gb1_2 = sb("gb1", [128, D])
            lng_2 = gb0_2; lnb_2 = gb1_2
            xT2 = sb("xT2", [128, 16, 128], BF16); xT2b = sb("xT2b", [128, 16, 128], BF16)
            wb_2 = [sb("wb%d" % i, [128, 2, 16, 256], BF16) for i in range(3)]
            ps_2 = [psb("ps%d" % i, [128, 512]) for i in range(5)]
            psO_l = [psb("psO%d" % i, [128, 512]) for i in range(2)]; psTb_2 = psb("psTb", [128, 1024], BF16)
            big_2 = [sb("big%d" % i, [128, 2048]) for i in range(2)]
            x1b_2 = sb("x1b", [128, D], BF16); oT_2 = sb("oT", [128, 16, 128], BF16); st8_2 = sb("st8", [128, 16])
            qT = sb("qT", [128, 16, 128], BF16)
            NKMAX = PAST + 64
            KTt = [sb("KTt%d" % i, [128, NKMAX], BF16) for i in range(4)]
            Pm_ = [sb("P%d" % i, [128, NKMAX], BF16) for i in range(4)]
            Vt2 = [sb("Vt%d" % i, [128, NKMAX // 128 + 1, 256], BF16) for i in range(2)]
            attnT = [sb("attnT%d" % i, [128, 8, 128], BF16) for i in range(2)]
            maskb = sb("maskb", [128, 256], BF16); selt = sb("selt", [128, 2]); lamt = sb("lamt", [128, 512]); lamc = sb("lamc", [128, 8]); sgw = sb("sgw", [128, 256])
            mx = sb("mx", [128, 2, 2, 16]); rs = sb("rs", [128, 2, 2, 16]); smt = sb("sm", [128, 2, 8])
            S.dma(maskb[:], mask2, writes=["maskb"], q="pool")
            S.dma(selt[:], sel_in, writes=["selt"])
            S.dma(lamt[:], lamv.partition_broadcast(128), writes=["lamt"])
            S.dma(sgw[:], subln_g.partition_broadcast(128), writes=["sgw"])
            S.dve(lambda e: e.tensor_scalar(out=sgw[:], in0=sgw[:], scalar1=1.0 - LAM_INIT, scalar2=None, op0=ALU.mult), reads=["sgw"], writes=["sgw"])
            for i in range(2):
                S.dve(lambda e, i=i: e.tensor_tensor(out=lamt[:, i * 256:i * 256 + 128], in0=lamt[:, i * 256:i * 256 + 128], in1=lamt[:, i * 256 + 128:i * 256 + 256], op=ALU.mult), reads=["lamt"], writes=["lamt"])
                S.dve(lambda e, i=i: e.tensor_reduce(out=lamc[:, 2 + i:3 + i], in_=lamt[:, i * 256:i * 256 + 128], axis=AX.X, op=ALU.add), reads=["lamt"], writes=["lamc"])
            S.act(lambda e: e.activation(out=lamc[:, 4:6], in_=lamc[:, 2:4], func=AF.Exp), reads=["lamc"], writes=["lamc"])
            S.dve(lambda e: e.tensor_tensor(out=lamc[:, 0:1], in0=lamc[:, 4:5], in1=lamc[:, 5:6], op=ALU.subtract), reads=["lamc"], writes=["lamc"])
            S.dve(lambda e: e.tensor_scalar(out=lamc[:, 1:2], in0=lamc[:, 0:1], scalar1=LAM_INIT, scalar2=-1.0, op0=ALU.add, op1=ALU.mult), reads=["lamc"], writes=["lamc"])

            wb_rr_2 = [0]; ps_rr_2 = [0]
            def load_gb_2(gsrc, bsrc, w):
                S.dma(gb0_2[:, 0:w], gsrc.partition_broadcast(128), writes=["gb0"])
                S.dma(gb1_2[:, 0:w], bsrc.partition_broadcast(128), writes=["gb1"])
            def wload_2(W, nm, c0, n):
                s_ = wb_rr_2[0] % len(wb_2); wb_rr_2[0] += 1
                assert c0 % 512 == 0 and n == 512
                S.dma(wb_2[s_][:, :, :, :], W[c0 // 256:c0 // 256 + 2].rearrange("b p a c -> p b a c"), reads=WRES(nm), writes=[("wb", s_)])
                return s_

            def nps_2():
                i = ps_rr_2[0] % len(ps_2); ps_rr_2[0] += 1
                return i

            def mm_2(out, lhsT, rhs, start, stop, reads, writes, tp=None):
                if tp is None:
                    S.pe(lambda e: e.matmul(out, lhsT=lhsT, rhs=rhs, start=start, stop=stop), reads, writes)
                else:
                    S.pe(lambda e: e.matmul(out, lhsT=lhsT, rhs=rhs, start=start, stop=stop, tile_position=tp), reads, writes)

            def ln_rows_2(src_ap, width, T, dst_ap, gt, bt_, eps, rsrc, rdst, gn, bn):
                junk = x1b_2
                S.dve(lambda e: e.memset(st8_2[0:T, 0:2], 0.0), writes=["st8a", "st8b"])
                S.act(lambda e: e.activation(out=junk[0:T, 0:width], in_=src_ap, func=AF.Identity, accum_out=st8_2[0:T, 0:1]), reads=rsrc + ["st8a"], writes=["x1b", "st8a"])
                S.act(lambda e: e.activation(out=junk[0:T, 0:width], in_=src_ap, func=AF.Square, accum_out=st8_2[0:T, 1:2]), reads=rsrc + ["x1b", "st8b"], writes=["x1b", "st8b"])
                S.dve(lambda e: e.tensor_scalar(out=st8_2[0:T, 2:3], in0=st8_2[0:T, 0:1], scalar1=1.0 / width, scalar2=None, op0=ALU.mult), reads=["st8a"], writes=["st8c"])
                S.dve(lambda e: e.tensor_tensor(out=st8_2[0:T, 3:4], in0=st8_2[0:T, 2:3], in1=st8_2[0:T, 2:3], op=ALU.mult), reads=["st8c"], writes=["st8d"])
                S.dve(lambda e: e.scalar_tensor_tensor(out=st8_2[0:T, 4:5], in0=st8_2[0:T, 1:2], scalar=1.0 / width, in1=st8_2[0:T, 3:4], op0=ALU.mult, op1=ALU.subtract), reads=["st8b", "st8d"], writes=["st8e"])
                S.dve(lambda e: e.tensor_scalar(out=st8_2[0:T, 4:5], in0=st8_2[0:T, 4:5], scalar1=eps, scalar2=None, op0=ALU.add), reads=["st8e"], writes=["st8e"])
                S.act(lambda e: e.activation(out=st8_2[0:T, 5:6], in_=st8_2[0:T, 4:5], func=AF.Sqrt), reads=["st8e"], writes=["st8f"])
                S.dve(lambda e: e.reciprocal(out=st8_2[0:T, 6:7], in_=st8_2[0:T, 5:6]), reads=["st8f"], writes=["st8g"])
                S.dve(lambda e: e.scalar_tensor_tensor(out=st8_2[0:T, 7:8], in0=st8_2[0:T, 2:3], scalar=-1.0, in1=st8_2[0:T, 6:7], op0=ALU.mult, op1=ALU.mult), reads=["st8c", "st8g"], writes=["st8h"])
                S.act(lambda e: e.activation(out=dst_ap, in_=src_ap, func=AF.Identity, bias=st8_2[0:T, 7:8], scale=st8_2[0:T, 6:7]), reads=rsrc + ["st8g", "st8h"], writes=rdst)
                S.dve(lambda e: e.tensor_tensor(out=dst_ap, in0=dst_ap, in1=gt[0:T, 0:width], op=ALU.mult), reads=rdst + ["gb0"], writes=rdst)
                S.dve(lambda e: e.tensor_tensor(out=dst_ap, in0=dst_ap, in1=bt_[0:T, 0:width], op=ALU.add), reads=rdst + ["gb1"], writes=rdst)


            def load_x1T(T, row0):
                S.dma(xT2[:, :, 0:T], x1T_all[:, row0:row0 + T].rearrange("(a p) t -> p a t", p=128), reads=[("x1T_all", row0 // 128)], writes=["xT2"])

            def proj_tok(T, c0, blk, evac):
                s_ = wload_2(Wc_in, "Wc_in", c0 + blk * 512, 512); pi = nps_2()
                for a in range(16):
                    mm_2(ps_2[pi][0:T, :], xT2[:, a, 0:T], wb_2[s_][:, :, a, :], a == 0, a == 15, ["xT2", ("wb", s_)], [("ps", pi)])
                evac(pi)

            def to_qT(T):
                for half in range(2):
                    for cc in range(8):
                        S.pe(lambda e, cc=cc, half=half: e.transpose(psTb_2[:, cc * 128:cc * 128 + T], x1b_2[0:T, (half * 8 + cc) * 128:(half * 8 + cc + 1) * 128], idb_2[0:T, 0:T]), reads=["x1b", "idb"], writes=["psTb"])
                    S.dve(lambda e, half=half: e.tensor_copy(out=qT[:, half * 8:half * 8 + 8, 0:T], in_=psTb_2[:].rearrange("p (a t) -> p a t", t=128)[:, :, 0:T]), reads=["psTb"], writes=[("qT", half)])
            QT = [("qT", 0), ("qT", 1)]

            def kv_tile(T, row0, k_out, v_out):
                load_x1T(T, row0)
                for which, c0, dst_out in ((0, 2048, k_out), (1, 4096, v_out)):
                    dst = big_2[which]; rn = ["big0a", "big0b"] if which == 0 else ["big1"]
                    for blk in range(4):
                        proj_tok(T, c0, blk, lambda pi, blk=blk, dst=dst, rn=rn: S.act(lambda e: e.copy(out=dst[0:T, blk * 512:(blk + 1) * 512], in_=ps_2[pi][0:T, :]), reads=[("ps", pi)], writes=rn))
                    S.dma(dst_out, dst[0:T, :], reads=rn)
                    S.dve(lambda e, dst=dst: e.tensor_copy(out=x1b_2[0:T, :], in_=dst[0:T, :]), reads=rn, writes=["x1b"])
                    if which == 0:
                        to_qT(T)
                        S.dma(KT[:, :, row0:row0 + T].rearrange("j p t -> p j t"), qT[:, :, 0:T], reads=QT, writes=[("KT", row0 // 128)])
                    else:
                        S.dma(Vb[row0:row0 + T, :], x1b_2[0:T, :], reads=["x1b"], writes=[("Vb", row0 // 128)])

            def kv_pair(rowA, rowB, outs2):
                T = 128
                S.dma(xT2[:, :, 0:T], x1T_all[:, rowA:rowA + T].rearrange("(a p) t -> p a t", p=128), reads=[("x1T_all", rowA // 128)], writes=["xT2"])
                S.dma(xT2b[:, :, 0:T], x1T_all[:, rowB:rowB + T].rearrange("(a p) t -> p a t", p=128), reads=[("x1T_all", rowB // 128)], writes=["xT2b"])
                srcs = ((xT2, "xT2", big_2[0], ["big0a", "big0b"]), (xT2b, "xT2b", big_2[1], ["big1"]))
                rows = (rowA, rowB)
                for which, c0 in ((0, 2048), (1, 4096)):
                    for blk in range(4):
                        s_ = wload_2(Wc_in, "Wc_in", c0 + blk * 512, 512)
                        for (xt_, xr, dst, rn) in srcs:
                            pi = nps_2()
                            for a in range(16):
                                mm_2(ps_2[pi][0:T, :], xt_[:, a, 0:T], wb_2[s_][:, :, a, :], a == 0, a == 15, [xr, ("wb", s_)], [("ps", pi)])
                            S.act(lambda e, pi=pi, dst=dst, blk=blk: e.copy(out=dst[0:T, blk * 512:(blk + 1) * 512], in_=ps_2[pi][0:T, :]), reads=[("ps", pi)], writes=rn)
                    for ti, (xt_, xr, dst, rn) in enumerate(srcs):
                        row0 = rows[ti]
                        S.dma(outs2[ti][which], dst[0:T, :], reads=rn)
                        S.pool(lambda e, dst=dst: e.tensor_copy(out=x1b_2[0:T, :], in_=dst[0:T, :]), reads=rn, writes=["x1b"])
                        if which == 0:
                            to_qT(T)
                            S.dma(KT[:, :, row0:row0 + T].rearrange("j p t -> p j t"), qT[:, :, 0:T], reads=QT, writes=[("KT", row0 // 128)])
                        else:
                            S.dma(Vb[row0:row0 + T, :], x1b_2[0:T, :], reads=["x1b"], writes=[("Vb", row0 // 128)])

            def att_tile(T, row0, nk_ctx, is_sample, y_out, rowB=None):
                load_x1T(T, row0)
                if rowB is not None:
                    S.dma(qT[:, :, 0:T], x1T_all[:, rowB:rowB + T].rearrange("(a p) t -> p a t", p=128), reads=[("x1T_all", rowB // 128)], writes=QT)
                    fl = lambda t_: t_[:].rearrange("p a t -> p (a t)")
                    S.dve(lambda e: e.tensor_scalar(out=fl(qT), in0=fl(qT), scalar1=selt[:, 1:2], scalar2=None, op0=ALU.mult), reads=QT + ["selt"], writes=QT)
                    S.dve(lambda e: e.scalar_tensor_tensor(out=fl(xT2), in0=fl(xT2), scalar=selt[:, 0:1], in1=fl(qT), op0=ALU.mult, op1=ALU.add), reads=QT + ["selt", "xT2"], writes=["xT2"])
                for blk in range(4):
                    proj_tok(T, 0, blk, lambda pi, blk=blk: S.act(lambda e: e.activation(out=x1b_2[0:T, blk * 512:(blk + 1) * 512], in_=ps_2[pi][0:T, :], func=AF.Copy, scale=float(128 ** -0.5)), reads=[("ps", pi)], writes=["x1b"]))
                to_qT(T)
                for blk in range(4):
                    proj_tok(T, 6144, blk, lambda pi, blk=blk: S.act(lambda e: e.activation(out=big_2[0][0:T, blk * 512:(blk + 1) * 512], in_=ps_2[pi][0:T, :], func=AF.Silu), reads=[("ps", pi)], writes=["big0a", "big0b"]))
                if is_sample:
                    nk = PAST + 64; nkt = PAST // 128 + 1
                else:
                    nk = nk_ctx; nkt = nk // 128
                nb = (nk + 511) // 512
                def stageA(h):
                    par = h % 2; hs_ = slice(h * 256, (h + 1) * 256); Vt = Vt2[par]; VR = ("Vt", par); sm = smt[:, par, :]
                    if is_sample:
                        S.dma(Vt[:, 0:PAST // 128, :], vc[:, hs_].rearrange("(kt p) e -> p kt e", p=128), writes=[VR], q="pool")
                        S.dma(Vt[0:64, PAST // 128, :], Vb[SEQ:SEQ + 64, hs_], reads=[("Vb", SEQ // 128)], writes=[VR])
                    else:
                        S.dma(Vt[:, 0:nkt, :], Vb[0:nk, hs_].rearrange("(kt p) e -> p kt e", p=128), reads=[("Vb", i) for i in range(nkt)], writes=[VR])
                    for jm in range(2):
                        hj = 2 * h + jm; kt_ = KTt[2 * par + jm]; KR = ("KTt", par, jm)
                        if is_sample:
                            S.dma(kt_[:, 0:PAST], kcT[hj], writes=[KR], q="pool")
                            S.dma(kt_[:, PAST:PAST + 64], KT[hj, :, SEQ:SEQ + 64], reads=[("KT", SEQ // 128)], writes=[KR])
                        else:
                            S.dma(kt_[:, 0:nk], KT[hj, :, 0:nk], reads=[("KT", i) for i in range(nkt)], writes=[KR])

                    def smm(pi, b, jm):
                        hj = 2 * h + jm; kt_ = KTt[2 * par + jm]; KR = ("KTt", par, jm)
                        w = min(512, nk - b * 512)
                        diag = (not is_sample) and (b == nb - 1)
                        mm_2(ps_2[pi][0:T, 0:w], qT[:, hj, 0:T], kt_[:, b * 512:b * 512 + w], True, not diag, [("qT", hj // 8), KR], [("ps", pi)])
                        if diag:
                            mm_2(ps_2[pi][0:T, w - 256:w], idb_2[0:T, 0:T], maskb[0:T, :], False, True, ["idb", "maskb"], [("ps", pi)])
                        return w
                    for b in range(nb):
                        for jm in range(2):
                            pi = nps_2(); w = smm(pi, b, jm)
                            S.dve(lambda e, pi=pi, w=w, b=b, jm=jm: e.tensor_reduce(out=mx[0:T, par, jm, b:b + 1], in_=ps_2[pi][0:T, 0:w], axis=AX.X, op=ALU.max), reads=[("ps", pi)], writes=[("mx", par, jm)])
                    for jm in range(2):
                        S.dve(lambda e, jm=jm: e.tensor_reduce(out=sm[0:T, jm:jm + 1], in_=mx[0:T, par, jm, 0:nb], axis=AX.X, op=ALU.max), reads=[("mx", par, jm)], writes=[("sm", par, jm)])
                        S.dve(lambda e, jm=jm: e.tensor_scalar(out=sm[0:T, jm:jm + 1], in0=sm[0:T, jm:jm + 1], scalar1=-1.0, scalar2=None, op0=ALU.mult), reads=[("sm", par, jm)], writes=[("sm", par, jm)])
                    for b in range(nb):
                        for jm in range(2):
                            pi = nps_2(); w = smm(pi, b, jm); Pm = Pm_[2 * par + jm]
                            S.act(lambda e, pi=pi, w=w, b=b, jm=jm, Pm=Pm: e.activation(out=Pm[0:T, b * 512:b * 512 + w], in_=ps_2[pi][0:T, 0:w], func=AF.Exp, bias=sm[0:T, jm:jm + 1], scale=1.0, accum_out=rs[0:T, par, jm, b:b + 1]),
                                  reads=[("ps", pi), ("sm", par, jm)], writes=[("P", par, jm), ("rs", par, jm)])
                    for jm in range(2):
                        S.dve(lambda e, jm=jm: e.tensor_reduce(out=sm[0:T, 2 + jm:3 + jm], in_=rs[0:T, par, jm, 0:nb], axis=AX.X, op=ALU.add), reads=[("rs", par, jm)], writes=[("sm", par, 2 + jm)])
                    P0 = Pm_[2 * par]; P1 = Pm_[2 * par + 1]
                    S.dve(lambda e: e.reciprocal(out=sm[0:T, 4:6], in_=sm[0:T, 2:4]), reads=[("sm", par, 2), ("sm", par, 3)], writes=[("sm", par, 4)])
                    S.dve(lambda e: e.tensor_tensor(out=sm[0:T, 5:6], in0=sm[0:T, 5:6], in1=lamc[0:T, 1:2], op=ALU.mult), reads=[("sm", par, 4), "lamc"], writes=[("sm", par, 4)])
                    S.dve(lambda e: e.tensor_scalar(out=P1[0:T, 0:nk], in0=P1[0:T, 0:nk], scalar1=sm[0:T, 5:6], scalar2=None, op0=ALU.mult), reads=[("P", par, 1), ("sm", par, 4)], writes=[("P", par, 1)])
                    S.dve(lambda e: e.scalar_tensor_tensor(out=P0[0:T, 0:nk], in0=P0[0:T, 0:nk], scalar=sm[0:T, 4:5], in1=P1[0:T, 0:nk], op0=ALU.mult, op1=ALU.add), reads=[("P", par, 0), ("P", par, 1), ("sm", par, 4)], writes=[("P", par, 0)])

                def stageB(h):
                    par = h % 2; hs_ = slice(h * 256, (h + 1) * 256); Vt = Vt2[par]; VR = ("Vt", par); sm = smt[:, par, :]
                    P0 = Pm_[2 * par]; psO = psO_l[par]; OR_ = ("psO", par); EP = ("ep", par); c0 = par * 512
                    for kt0 in range(0, nkt, 8):
                        n8 = min(8, nkt - kt0); ai = (kt0 // 8) % 2; aT = attnT[ai]
                        for i in range(n8):
                            kt = kt0 + i; kw = min(128, nk - kt * 128)
                            S.pe(lambda e, i=i, kt=kt, kw=kw: e.transpose(psTb_2[0:kw, i * 128:i * 128 + T], P0[0:T, kt * 128:kt * 128 + kw], idb_2[0:T, 0:T]), reads=[("P", par, 0), "idb"], writes=["psTb"])
                        S.act(lambda e, n8=n8, aT=aT: e.copy(out=aT[:, 0:n8, 0:T], in_=psTb_2[:].rearrange("p (a t) -> p a t", t=128)[:, 0:n8, 0:T]), reads=["psTb"], writes=[("attnT", ai)])
                        for i in range(n8):
                            kt = kt0 + i; kw = min(128, nk - kt * 128)
                            mm_2(psO[0:T, 0:256], aT[0:kw, i, 0:T], Vt[0:kw, kt, :], kt == 0, kt == nkt - 1, [("attnT", ai), VR], [OR_])
                    S.act(lambda e: e.activation(out=big_2[1][0:T, c0:c0 + 256], in_=psO[0:T, 0:256], func=AF.Square, accum_out=sm[0:T, 6:7]), reads=[OR_], writes=[EP, ("sm", par, 6)])
                    S.dve(lambda e: e.tensor_scalar(out=sm[0:T, 6:7], in0=sm[0:T, 6:7], scalar1=1.0 / 256, scalar2=1e-5, op0=ALU.mult, op1=ALU.add), reads=[("sm", par, 6)], writes=[("sm", par, 6)])
                    S.act(lambda e: e.activation(out=sm[0:T, 6:7], in_=sm[0:T, 6:7], func=AF.Sqrt), reads=[("sm", par, 6)], writes=[("sm", par, 6)])
                    S.dve(lambda e: e.reciprocal(out=sm[0:T, 7:8], in_=sm[0:T, 6:7]), reads=[("sm", par, 6)], writes=[("sm", par, 7)])
                    S.act(lambda e: e.activation(out=big_2[1][0:T, c0 + 256:c0 + 512], in_=psO[0:T, 0:256], func=AF.Copy, scale=sm[0:T, 7:8]), reads=[OR_, ("sm", par, 7), EP], writes=[EP])
                    S.dve(lambda e: e.tensor_tensor(out=big_2[1][0:T, c0 + 256:c0 + 512], in0=big_2[1][0:T, c0 + 256:c0 + 512], in1=sgw[0:T, :], op=ALU.mult), reads=[EP, "sgw"], writes=[EP])
                    S.dve(lambda e: e.tensor_tensor(out=x1b_2[0:T, hs_], in0=big_2[1][0:T, c0 + 256:c0 + 512], in1=big_2[0][0:T, hs_], op=ALU.mult), reads=[EP, "big0a", "big0b"], writes=["x1b"])

                stageA(0)
                for h in range(8):
                    if h + 1 < 8:
                        stageA(h + 1)
                    stageB(h)
                for half in range(2):
                    for cc in range(8):
                        S.pe(lambda e, cc=cc, half=half: e.transpose(psTb_2[:, cc * 128:cc * 128 + T], x1b_2[0:T, (half * 8 + cc) * 128:(half * 8 + cc + 1) * 128], idb_2[0:T, 0:T]), reads=["x1b", "idb"], writes=["psTb"])
                    S.dve(lambda e, half=half: e.tensor_copy(out=oT_2[:, half * 8:half * 8 + 8, 0:T], in_=psTb_2[:].rearrange("p (a t) -> p a t", t=128)[:, :, 0:T]), reads=["psTb"], writes=[("oT", half)])
                OT2 = [("oT", 0), ("oT", 1)]
                S.dma(big_2[1][0:T, :], x1_all[row0:row0 + T, :], reads=[("x1_all", row0 // 128)], writes=["big1", ("ep", 0), ("ep", 1)])
                if rowB is not None:
                    S.dma(big_2[0][0:T, :], x1_all[rowB:rowB + T, :], reads=[("x1_all", rowB // 128)], writes=["big0a", "big0b"])
                    S.dve(lambda e: e.tensor_scalar(out=big_2[0][0:T, :], in0=big_2[0][0:T, :], scalar1=selt[0:T, 1:2], scalar2=None, op0=ALU.mult), reads=["big0a", "big0b", "selt"], writes=["big0a", "big0b"])
                    S.dve(lambda e: e.scalar_tensor_tensor(out=big_2[1][0:T, :], in0=big_2[1][0:T, :], scalar=selt[0:T, 0:1], in1=big_2[0][0:T, :], op0=ALU.mult, op1=ALU.add), reads=["big0a", "big0b", "big1", "selt"], writes=["big1"])
                for blk in range(4):
                    sw = wload_2(Wc_out, "Wc_out", blk * 512, 512); po = nps_2()
                    for a in range(16):
                        mm_2(ps_2[po][0:T, :], oT_2[:, a, 0:T], wb_2[sw][:, :, a, :], a == 0, a == 15, OT2 + [("wb", sw)], [("ps", po)])
                    S.dve(lambda e, po=po, blk=blk: e.scalar_tensor_tensor(out=big_2[1][0:T, blk * 512:(blk + 1) * 512], in0=big_2[1][0:T, blk * 512:(blk + 1) * 512], scalar=ALPHA, in1=ps_2[po][0:T, :], op0=ALU.mult, op1=ALU.add),
                          reads=[("ps", po), "big1"], writes=["big1"])
                ln_rows_2(big_2[1][0:T, :], D, T, big_2[0][0:T, :], lng_2, lnb_2, 1e-5, ["big1"], ["big0a", "big0b"], "lng", "lnb")
                S.dma(y_out, big_2[0][0:T, :], reads=["big0a", "big0b"])

            for i in range(0, NT, 2):
                kv_pair(i * 128, (i + 1) * 128, [(k_p[r * 128:(r + 1) * 128, :], v_p[r * 128:(r + 1) * 128, :]) for r in (i, i + 1)])
            kv_tile(64, SEQ, k_s, v_s)
            load_gb_2(cln_g, cln_b, D)
            for j in range(NT // 2):
                att_tile(128, 2 * j * 128, (2 * j + 2) * 128, False, y_p[j * 128:(j + 1) * 128, :], rowB=(2 * j + 1) * 128)
            att_tile(64, SEQ, 0, True, y_s)
            print("SBUF remaining L1", nc.sbuf_bytes_remaining)
            info = S.emit()
    import os
    if os.environ.get("K_MARKS"): print(MARKS[:40])
    return nc, info


_CACHE = {}
NCORES = 8
DO_L1 = True


def kernel(**inp):
    f = lambda a: np.ascontiguousarray(np.asarray(a, dtype=np.float32))
    do_l1 = DO_L1
    if "nc" not in _CACHE:
        _CACHE["nc"] = build(do_l1)
    nc, info = _CACHE["nc"]
    xp = f(inp["x_prompt"]); xs = f(inp["x_sample"])
    common = {
        "w_in": f(np.concatenate([np.asarray(inp["ab_w_in"][0])[:, 0:6144], np.asarray(inp["ab_w_in"][0])[:, 6272:7296], np.asarray(inp["ab_w_in"][0])[:, 6144:6272]], axis=1)), "w_out": f(inp["ab_w_out"][0]), "cw_in": f(inp["c_w_in"][0]), "cw_out": f(inp["c_w_out"][0]),
        "a_ln_g": f(inp["ab_a_ln_g"]), "a_ln_b": f(inp["ab_a_ln_b"]),
        "a_wsT": f(np.transpose(np.asarray(inp["ab_a_ws"][0]), (0, 2, 1))), "a_bs": f(np.asarray(inp["ab_a_bs"][0]).reshape(1, 512)),
        "b_mu": f(inp["ab_b_mu"][0]), "b_w0": f(inp["ab_b_w0"][0]), "b_w2": f(inp["ab_b_w2"][0]), "b_a0": f(inp["ab_b_a0"][0]),
        "b_a2": f(inp["ab_b_a2"][0]), "b_kk": f(inp["ab_b_kk"][0]), "b_ka": f(inp["ab_b_ka"][0]), "b_rk": f(np.asarray(inp["ab_b_rk"][0]).reshape(1024)),
        "b_lnx_g": f(inp["ab_b_lnx_g"]), "b_lnx_b": f(inp["ab_b_lnx_b"]),
        "ln_g": f(inp["ab_ln_g"]), "ln_b": f(inp["ab_ln_b"]), "cln_g": f(inp["c_ln_g"]), "cln_b": f(inp["c_ln_b"]),
        "lamv": f(np.concatenate([np.asarray(inp[k][0]) for k in ("c_lam_q1", "c_lam_k1", "c_lam_q2", "c_lam_k2")]).reshape(1, 512)),
        "subln_g": f(inp["c_subln_g"]),
    }
    ck = np.asarray(inp["cache_c_k"][0]); cv = np.asarray(inp["cache_c_v"][0])
    in_maps = []
    NB = max(1, NCORES // 2)
    for c in range(NCORES):
        b = c // 2
        m = dict(common)
        md = np.where((np.arange(128)[None, :] // 64) <= (np.arange(128)[:, None] // 64), 0.0, -30000.0)
        pp = c % 2
        m["mask2"] = f(np.concatenate([md, np.full((128, 128), -30000.0)], 1) if pp == 0 else np.concatenate([np.zeros((128, 128)), md], 1))
        m["sel"] = f(np.tile(np.array([[1.0 - pp, float(pp)]]), (128, 1)))
        m["xT_p"] = f(xp[b, :SEQ].T); m["x_p"] = f(xp[b, :SEQ]); m["xT_s"] = f(xs[c].T); m["x_s"] = f(xs[c])
        m["st_shift"] = f(inp["state_b_shift"][0, c]); m["st_wkvT"] = f(np.transpose(np.asarray(inp["state_b_wkv"][0, c]), (0, 2, 1)))
        if DO_L1:
            m["kcT"] = f(np.transpose(ck[c].reshape(PAST, 16, 128), (1, 2, 0))); m["vc"] = f(cv[c].reshape(PAST, 2048))
        in_maps.append(m)
    import os
    if os.environ.get("K_TRACE"):
        res = run_bass_kernel_spmd(nc, in_maps, core_ids=list(range(NCORES)), trace=True)
        print("EXEC_TIME_NS", res.exec_time_ns)
        global LAST_RES
        LAST_RES = res
    else:
        res = run_bass_kernel_spmd(nc, in_maps, core_ids=list(range(NCORES)))
    R = res.results
    global LAST_R
    LAST_R = R
    T3 = lambda a: np.transpose(a, (0, 2, 1))
    if DO_L1:
        y_prompt = np.stack([np.stack([R[2 * b + (1 if NCORES > 1 else 0) * q]["y_p"].reshape(SEQ // 256, 128, D) for q in range(2)], axis=1).reshape(SEQ, D) for b in range(NB)])
    else:
        y_prompt = np.stack([R[2 * b]["y_p"] for b in range(NB)])
    y_sample = np.stack([R[c]["y_s"] for c in range(NCORES)])
    sh_p = np.stack([R[2 * b]["sh_p"] for b in range(NB)])[None]
    wkv_p = np.stack([T3(R[2 * b]["wkvT_p"]) for b in range(NB)])[None]
    sh_s = np.stack([R[c]["sh_s"] for c in range(NCORES)])[None]
    wkv_s = np.stack([T3(R[c]["wkvT_s"]) for c in range(NCORES)])[None]
    va_s = np.stack([R[c]["va_s"] for c in range(NCORES)])[None]
    k_p = np.stack([R[2 * b]["k_p"] for b in range(NB)]).reshape(1, NB, SEQ, 8, 2, 128)
    v_p = np.stack([R[2 * b]["v_p"] for b in range(NB)]).reshape(1, NB, SEQ, 8, 256)
    k_s = np.stack([R[c]["k_s"] for c in range(NCORES)]).reshape(1, NCORES, 64, 8, 2, 128)
    v_s = np.stack([R[c]["v_s"] for c in range(NCORES)]).reshape(1, NCORES, 64, 8, 256)
    return (y_prompt, y_sample, sh_p, wkv_p, sh_s, wkv_s, va_s, k_p, v_p, k_s, v_s)
```
